# Optimizing a Trainium2 kernel written in Bass

```python
import math
import jax, jax.numpy as jnp
from jax import lax
import numpy as np

D_MODEL = 1024
BATCH = 4
SEQ = 8192
DEPTH = 2

DIFF_HEADS = 4
DIFF_HEAD_DIM = 64
DIFF_WIDTH = DIFF_HEADS * 2 * DIFF_HEAD_DIM
SB_HEADS = 8
SB_HEAD_DIM = 64
SB_WIDTH = SB_HEADS * SB_HEAD_DIM
IN_COLS = 3 * DIFF_WIDTH + 3 * SB_WIDTH + 2 * D_MODEL
FFN_HIDDEN = -(-(8 * D_MODEL) // (3 * 256)) * 256
Q_BLOCK = 128
ROPE_THETA = 10000.0
NORM_EPS = 1e-6

kernel_name = "hybrid_diffattn_stickbreaking_gated"


def rms_norm(x, g):
    xf = x.astype(jnp.float32)
    y = xf * lax.rsqrt(jnp.mean(xf * xf, axis=-1, keepdims=True) + NORM_EPS)
    return (y * g.astype(jnp.float32)).astype(x.dtype)


def rope_tables(seq, dim):
    pos = jnp.arange(seq, dtype=jnp.float32)
    inv = ROPE_THETA ** (-jnp.arange(0, dim, 2, dtype=jnp.float32) / dim)
    ang = pos[:, None] * inv[None, :]
    return jnp.cos(ang), jnp.sin(ang)


def apply_rope(t, cos, sin):
    c = cos[None, :, None, :].astype(t.dtype)
    s = sin[None, :, None, :].astype(t.dtype)
    t1, t2 = jnp.split(t, 2, axis=-1)
    return jnp.concatenate([t1 * c - t2 * s, t2 * c + t1 * s], axis=-1)


def to_blocks(t):
    b, h, s, d = t.shape
    return t.reshape(b, h, s // Q_BLOCK, Q_BLOCK, d).transpose(2, 0, 1, 3, 4)


def from_blocks(o):
    nb, b, h, q, dv = o.shape
    return o.transpose(1, 0, 3, 2, 4).reshape(b, nb * q, h, dv)


def diff_attention(q1, q2, k1, k2, v, lam):
    s_len = k1.shape[2]
    kpos = jnp.arange(s_len)
    qpos_blocks = kpos.reshape(-1, Q_BLOCK)
    scale = DIFF_HEAD_DIM ** -0.5

    def block(args):
        qb1, qb2, qpos = args
        mask = kpos[None, :] <= qpos[:, None]

        def probs(qb, k):
            sc = jnp.einsum('bhqd,bhkd->bhqk', qb, k).astype(jnp.float32) * scale
            sc = jnp.where(mask, sc, -jnp.inf)
            return jax.nn.softmax(sc, axis=-1)

        w = probs(qb1, k1) - lam * probs(qb2, k2)
        return jnp.einsum('bhqk,bhkd->bhqd', w.astype(v.dtype), v)

    o = lax.map(block, (to_blocks(q1), to_blocks(q2), qpos_blocks))
    return from_blocks(o)


def stick_breaking_attention(q, k, v):
    s_len = k.shape[2]
    kpos = jnp.arange(s_len)
    qpos_blocks = kpos.reshape(-1, Q_BLOCK)
    scale = SB_HEAD_DIM ** -0.5

    def block(args):
        qb, qpos = args
        mask = kpos[None, :] < qpos[:, None]
        z = jnp.einsum('bhqd,bhkd->bhqk', qb, k).astype(jnp.float32) * scale
        log_beta = jax.nn.log_sigmoid(z)
        log_rem = jnp.where(mask, jax.nn.log_sigmoid(-z), 0.0)
        key_axis = log_rem.ndim - 1
        after = lax.cumsum(log_rem, axis=key_axis, reverse=True) - log_rem
        a = jnp.where(mask, jnp.exp(log_beta + after), 0.0)
        return jnp.einsum('bhqk,bhkd->bhqd', a.astype(v.dtype), v)

    o = lax.map(block, (to_blocks(q), qpos_blocks))
    return from_blocks(o)


def hybrid_layer(x, cos, sin, layer, g_attn, w_in, b_gate, lam_p, subln, w_o_diff, w_o_sb,
                 w_out, g_ffn, w_ffn_in, w_ffn_out):
    b, s, _ = x.shape
    h = rms_norm(x, g_attn)
    proj = h @ w_in
    offs = np.cumsum([DIFF_WIDTH, DIFF_WIDTH, DIFF_WIDTH, SB_WIDTH, SB_WIDTH, SB_WIDTH]).tolist()
    dq, dk, dv, sq, sk, sv, gates = jnp.split(proj, offs, axis=-1)

    dq = apply_rope(dq.reshape(b, s, 2 * DIFF_HEADS, DIFF_HEAD_DIM), cos, sin)
    dk = apply_rope(dk.reshape(b, s, 2 * DIFF_HEADS, DIFF_HEAD_DIM), cos, sin)
    dq = dq.reshape(b, s, DIFF_HEADS, 2, DIFF_HEAD_DIM).transpose(3, 0, 2, 1, 4)
    dk = dk.reshape(b, s, DIFF_HEADS, 2, DIFF_HEAD_DIM).transpose(3, 0, 2, 1, 4)
    dv = dv.reshape(b, s, DIFF_HEADS, 2 * DIFF_HEAD_DIM).transpose(0, 2, 1, 3)
    lam_init = 0.8 - 0.6 * math.exp(-0.3 * layer)
    lp = lam_p.astype(jnp.float32)
    lam = jnp.exp(jnp.sum(lp[0] * lp[1])) - jnp.exp(jnp.sum(lp[2] * lp[3])) + lam_init
    o_diff = diff_attention(dq[0], dq[1], dk[0], dk[1], dv, lam)
    o_diff = (rms_norm(o_diff, subln) * (1.0 - lam_init)).reshape(b, s, DIFF_WIDTH)

    sq = sq.reshape(b, s, SB_HEADS, SB_HEAD_DIM).transpose(0, 2, 1, 3)
    sk = sk.reshape(b, s, SB_HEADS, SB_HEAD_DIM).transpose(0, 2, 1, 3)
    sv = sv.reshape(b, s, SB_HEADS, SB_HEAD_DIM).transpose(0, 2, 1, 3)
    o_sb = stick_breaking_attention(sq, sk, sv).reshape(b, s, SB_WIDTH)

    g_diff, g_sb = jnp.split(jax.nn.sigmoid(gates + b_gate), 2, axis=-1)
    merged = g_diff * (o_diff @ w_o_diff) + g_sb * (o_sb @ w_o_sb)
    x = x + merged @ w_out

    h2 = rms_norm(x, g_ffn)
    gate, up = jnp.split(h2 @ w_ffn_in, 2, axis=-1)
    return x + (jax.nn.silu(gate) * up) @ w_ffn_out


def setup_inputs(seed: int = 0) -> dict:
    key = jax.random.key(seed)
    ks = jax.random.split(key, 14)
    f32 = jnp.float32
    nrm = lambda k, shape, scale: jax.random.normal(k, shape, f32) * scale
    return {
        "x": nrm(ks[0], (BATCH, SEQ, D_MODEL), 1.0),
        "norm_attn": 1.0 + nrm(ks[1], (DEPTH, D_MODEL), 0.02),
        "w_in": nrm(ks[2], (DEPTH, D_MODEL, IN_COLS), D_MODEL ** -0.5),
        "b_gate": nrm(ks[3], (DEPTH, 2 * D_MODEL), 0.02),
        "diff_lambda": nrm(ks[4], (DEPTH, 4, DIFF_HEAD_DIM), 0.1),
        "diff_subln": 1.0 + nrm(ks[5], (DEPTH, 2 * DIFF_HEAD_DIM), 0.02),
        "w_o_diff": nrm(ks[6], (DEPTH, DIFF_WIDTH, D_MODEL), DIFF_WIDTH ** -0.5),
        "w_o_sb": nrm(ks[7], (DEPTH, SB_WIDTH, D_MODEL), SB_WIDTH ** -0.5),
        "w_out": nrm(ks[8], (DEPTH, D_MODEL, D_MODEL), D_MODEL ** -0.5),
        "norm_ffn": 1.0 + nrm(ks[9], (DEPTH, D_MODEL), 0.02),
        "w_ffn_in": nrm(ks[10], (DEPTH, D_MODEL, 2 * FFN_HIDDEN), D_MODEL ** -0.5),
        "w_ffn_out": nrm(ks[11], (DEPTH, FFN_HIDDEN, D_MODEL), FFN_HIDDEN ** -0.5),
        "norm_final": 1.0 + nrm(ks[12], (D_MODEL,), 0.02),
    }


def reference(x, norm_attn, w_in, b_gate, diff_lambda, diff_subln, w_o_diff, w_o_sb, w_out,
              norm_ffn, w_ffn_in, w_ffn_out, norm_final):
    cos, sin = rope_tables(x.shape[1], DIFF_HEAD_DIM)
    for layer in range(DEPTH):
        x = hybrid_layer(x, cos, sin, layer, norm_attn[layer], w_in[layer], b_gate[layer],
                         diff_lambda[layer], diff_subln[layer], w_o_diff[layer], w_o_sb[layer],
                         w_out[layer], norm_ffn[layer], w_ffn_in[layer], w_ffn_out[layer])
    return rms_norm(x, norm_final)
```

```python
import math
from contextlib import ExitStack

import ml_dtypes
import numpy as np

import concourse.bass as bass
import concourse.mybir as mybir
from concourse.bass_utils import run_bass_kernel_spmd

F32 = mybir.dt.float32
BF16 = mybir.dt.bfloat16
AF = mybir.ActivationFunctionType
ALU = mybir.AluOpType
NPBF = ml_dtypes.bfloat16

D = 1024
FFN = 2816
NF = FFN // 128
EPS = 1e-6
N_CORES = 8


class Op:
    __slots__ = ("eng", "fn", "waits", "signal", "value", "ndma", "sem", "is_dma")


class Prog:
    ENGS = ("sp", "act", "pool", "dve", "pe")

    def __init__(self, nc, stack, n_dma_sems=100):
        self.nc = nc
        self.q = {e: [] for e in self.ENGS}
        self.esem = {e: stack.enter_context(nc.semaphore("es_" + e)) for e in ("act", "pool", "dve", "pe")}
        self.ecount = {e: 0 for e in self.esem}
        self.stack = stack
        self.dma_sem_of = {}
        self.dma_count = {}
        self.last_writer = {}
        self.readers = {}
        self.waited = {e: {} for e in self.ENGS}

    def _dsem(self, key):
        if key not in self.dma_sem_of:
            idx = len(self.dma_sem_of)
            self.dma_sem_of[key] = self.stack.enter_context(self.nc.semaphore("ds%d" % idx))
            self.dma_count[key] = 0
        return self.dma_sem_of[key]

    def op(self, eng, fn, reads=(), writes=(), sem=None, ndma=0):
        o = Op()
        o.eng = eng
        o.fn = fn
        o.signal = False
        o.value = None
        o.ndma = ndma
        o.is_dma = ndma > 0
        o.sem = None
        deps = []
        for k in reads:
            lw = self.last_writer.get(k)
            if lw is not None:
                deps.append(lw)
        for k in writes:
            lw = self.last_writer.get(k)
            if lw is not None:
                deps.append(lw)
            rs = self.readers.get(k)
            if rs:
                deps.extend(rs)
        waits = []
        seen = set()
        for d in deps:
            if id(d) in seen:
                continue
            seen.add(id(d))
            if (not d.is_dma) and d.eng == "pe" and eng == "pe" and not o.is_dma:
                continue
            if not d.is_dma:
                d.signal = True
            waits.append(d)
        o.waits = waits
        if o.is_dma:
            o.sem = self._dsem(sem)
            self.dma_count[sem] += 16 * ndma
            o.value = self.dma_count[sem]
        for k in writes:
            self.last_writer[k] = o
            self.readers[k] = []
        for k in reads:
            self.readers.setdefault(k, []).append(o)
        self.q[eng].append(o)
        return o

    def flush(self):
        nc = self.nc
        for e in self.esem:
            c = self.ecount[e]
            for o in self.q[e]:
                if o.is_dma:
                    continue
                if o.signal:
                    c += 1
                    o.value = c
            self.ecount[e] = c
        dma_final = [(self.dma_sem_of[k], self.dma_count[k]) for k in self.dma_sem_of]

        def emit(e, eng):
            wd = self.waited[eng]
            for o in self.q[eng]:
                for d in o.waits:
                    s = d.sem if d.is_dma else self.esem[d.eng]
                    v = d.value
                    if wd.get(id(s), 0) < v:
                        e.wait_ge(s, v)
                        wd[id(s)] = v
                if o.fn is None:
                    continue
                r = o.fn(e)
                if o.is_dma:
                    if not isinstance(r, (list, tuple)):
                        r = [r]
                    assert len(r) == o.ndma, (len(r), o.ndma)
                    for ins in r:
                        ins.then_inc(o.sem, 16)
                elif o.signal:
                    r.then_inc(self.esem[eng], 1)
            if eng == "sp":
                for s, v in dma_final:
                    if v > 0 and wd.get(id(s), 0) < v:
                        e.wait_ge(s, v)
                        wd[id(s)] = v

        with nc.Block() as block:
            decos = {"sp": block.sync, "act": block.scalar, "pool": block.gpsimd, "dve": block.vector,
                     "pe": block.tensor}
            for eng in self.ENGS:
                if self.q[eng] or eng == "sp":
                    decos[eng](lambda e, eng=eng: emit(e, eng))
        self.q = {e: [] for e in self.ENGS}
        self.last_writer = {}
        self.readers = {}


class Rot:
    def __init__(self, n):
        self.n = n
        self.i = -1

    def next(self):
        self.i = (self.i + 1) % self.n
        return self.i


def bcast_ap(handle, n, offset=0):
    return bass.AP(handle, offset, [[0, 128], [1, n]])


C_ID, C_TRI, C_ONE, C_MS, C_MI, NCONST = 0, 128, 256, 384, 1280, 2176


def host_consts():
    c = np.zeros((128, NCONST), np.float32)
    j = np.arange(128)[:, None]
    c[:, C_ID:C_ID + 128] = (j == np.arange(128)[None, :])
    c[:, C_TRI:C_TRI + 128] = (j >= np.arange(128)[None, :])
    c[:, C_ONE:C_ONE + 128] = 1.0
    u = np.arange(896)[None, :] - 384
    c[:, C_MS:C_MS + 896] = (u - j > 0)
    c[:, C_MI:C_MI + 896] = (u - j >= 0)
    return c.astype(NPBF)


def rope_tables(S):
    pos = np.arange(S, dtype=np.float32)
    inv = (np.float32(10000.0) ** (-np.arange(0, 64, 2, dtype=np.float32) / np.float32(64))).astype(np.float32)
    ang = (pos[:, None] * inv[None, :]).astype(np.float32)
    cos = np.cos(ang).astype(np.float32).T
    sin = np.sin(ang).astype(np.float32).T
    cosT = np.ascontiguousarray(np.tile(cos, (4, 1)))
    sinT = np.ascontiguousarray(np.tile(sin, (4, 1)))
    return cosT, sinT


class Alloc:
    _n = [0]

    def __init__(self, nc, st, prefix):
        Alloc._n[0] += 1
        self.nc, self.st, self.prefix = nc, st, "p%d_%s" % (Alloc._n[0], prefix)

    def sb(self, name, shape, dt):
        return self.st.enter_context(self.nc.sbuf_tensor(self.prefix + name, shape, dt))

    def ps(self, name, shape, dt):
        return self.st.enter_context(self.nc.psum_tensor(self.prefix + name, shape, dt))


def phase_A1(P, nc, NT, x_d, g_h, consts_d, hT_d, g_off=0):
    with ExitStack() as st:
        A = Alloc(nc, st, "a1_")
        xt = [A.sb("x%d" % i, [128, 1024], F32) for i in range(3)]
        gb = A.sb("gb", [128, 1024], F32)
        junk = A.sb("junk", [128, 1024], BF16)
        ssq = A.sb("ssq", [128, 1], F32)
        rstd = A.sb("rstd", [128, 1], F32)
        hb = A.sb("hb", [128, 1024], BF16)
        cst = A.sb("cst", [128, NCONST], BF16)
        hTs = [A.sb("hT%d" % i, [128, 8, 512], BF16) for i in range(2)]
        psT = A.ps("psT", [128, 1024], BF16)
        P.op("sp", lambda e: e.dma_start(out=cst[:], in_=consts_d), writes=["consts"], sem="cst", ndma=1)
        P.op("sp", lambda e: e.dma_start(out=gb[:], in_=bcast_ap(g_h, 1024, g_off)), writes=["gb"], sem="gb", ndma=1)
        hT_v = hT_d.rearrange("(c p) t -> p c t", p=128)
        for i in range(NT // 128):
            s = i % 3
            P.op("sp", lambda e, i=i, s=s: e.dma_start(out=xt[s][:], in_=x_d[i * 128:(i + 1) * 128, :]),
                 writes=["x%d" % s], sem="x%d" % s, ndma=1)
            hs, sub = (i // 4) % 2, i % 4
            emit_norm_transpose2(P, ["x%d" % s], xt[s][:], gb, junk, ssq, rstd, hb, psT, cst[:, C_ID:C_ID + 128],
                                hTs[hs][:, :, sub * 128:(sub + 1) * 128], [("hTs", hs, sub)])
            if sub == 3:
                t0 = (i // 4) * 512
                P.op("act", lambda e, hs=hs, t0=t0: e.dma_start(out=hT_v[:, :, t0:t0 + 512], in_=hTs[hs][:]),
                     reads=[("hTs", hs, k) for k in range(4)], sem="hTs%d" % hs, ndma=1)
        P.flush()


WA_DQ, WA_DQR, WA_DK, WA_DKR, WA_SQ, WA_SK, WA_V, WA_G, WA_N = 0, 256, 512, 768, 1024, 1280, 1536, 2048, 4096


def phase_A2(P, nc, S, hT_all_d, hT_mine_d, wqk_d, wv_d, wg_d, bg_d, cos_d, sin_d, QK_d, V_d, gT_d, il=None):
    NT = S // 2
    with ExitStack() as st:
        A = Alloc(nc, st, "a2_")
        WA = A.sb("WA", [128, 8, WA_N], BF16)
        stg = [A.sb("stg%d" % i, [128, 8, 512], F32) for i in range(2)]
        ht = [A.sb("ht%d" % i, [128, 8, 512], BF16) for i in range(2)]
        cs = [A.sb("cs%d" % i, [128, 2, 512], F32) for i in range(2)]
        t1 = [A.sb("t1_%d" % i, [128, 512], F32) for i in range(2)]
        t2 = [A.sb("t2_%d" % i, [128, 512], F32) for i in range(2)]
        QKs = [A.sb("QKs%d" % i, [128, 8, 512], BF16) for i in range(2)]
        Vs = [A.sb("Vs%d" % i, [128, 4, 512], BF16) for i in range(2)]
        Gs = [A.sb("Gs%d" % i, [128, 8, 512], BF16) for i in range(2)]
        bg = A.sb("bg", [128, 16], F32)
        ps = A.ps("ps", [128, 6, 512], F32)
        bank = Rot(6)
        P.op("sp", lambda e: e.dma_start(out=bg[:], in_=bg_d), writes=["bg"], sem="bg", ndma=1)

        def wview(w_d):
            return w_d.rearrange("(k p) c -> p k c", p=128)
        groups = [(wview(wqk_d), 0), (wview(wqk_d), 512), (wview(wv_d), 0)] + [(wview(wg_d), i * 512) for i in range(4)]
        ceng = Rot(2)
        for gi, (wv_, c0) in enumerate(groups):
            s = gi % 2
            P.op("sp", lambda e, wv_=wv_, c0=c0, s=s: e.dma_start(out=stg[s][:], in_=wv_[:, :, c0:c0 + 512]),
                 writes=[("stg", s)], sem="stg%d" % s, ndma=1)

            def cast(dst0, src0, n, scale, s=s):
                eng = ("dve", "pool")[ceng.next()]
                P.op(eng, lambda e: e.tensor_scalar(out=WA[:, :, dst0:dst0 + n], in0=stg[s][:, :, src0:src0 + n],
                                                    scalar1=float(scale), scalar2=None, op0=ALU.mult),
                     reads=[("stg", s)], writes=[("WA", dst0)])
            if gi == 0:
                cast(WA_DQ, 0, 256, 0.125)
                cast(WA_DK, 256, 256, 1.0)
                for sec, src, sc in ((WA_DQR, 0, 0.125), (WA_DKR, 256, 1.0)):
                    for b in range(4):
                        cast(sec + b * 64, src + b * 64 + 32, 32, -sc)
                        cast(sec + b * 64 + 32, src + b * 64, 32, sc)
            elif gi == 1:
                cast(WA_SQ, 0, 256, 0.125)
                cast(WA_SK, 256, 256, 1.0)
            elif gi == 2:
                cast(WA_V, 0, 512, 1.0)
            else:
                cast(WA_G + (gi - 3) * 512, 0, 512, 1.0)
        WAK = [k for k in P.last_writer if isinstance(k, tuple) and k[0] == "WA"]

        QK_v = QK_d.rearrange("(c p) t -> p c t", p=128)
        V_v = V_d.rearrange("(s p) c -> p s c", p=128)
        ntt = NT // 512
        def ld_main(T):
            s = T % 2
            r, tt = T // ntt, T % ntt
            hv = hT_all_d[r].rearrange("(c p) t -> p c t", p=128)
            P.op("sp", lambda e, hv=hv, tt=tt, s=s: e.dma_start(out=ht[s][:], in_=hv[:, :, tt * 512:(tt + 1) * 512]),
                 writes=[("ht", s)], sem="ht%d" % s, ndma=1)
            P.op("sp", lambda e, T=T, s=s: [e.dma_start(out=cs[s][:, 0, :], in_=cos_d[:, T * 512:(T + 1) * 512]),
                                            e.dma_start(out=cs[s][:, 1, :], in_=sin_d[:, T * 512:(T + 1) * 512])],
                 writes=[("cs", s)], sem="cs%d" % s, ndma=2)

        ld_main(0)
        for T in range(S // 512):
            s = T % 2
            if T + 1 < S // 512:
                ld_main(T + 1)

            def mm8(b, col0, s=s):
                for k in range(8):
                    P.op("pe", lambda e, k=k: e.matmul(ps[:, b, :], lhsT=WA[:, k, col0:col0 + 128], rhs=ht[s][:, k, :],
                                                       start=(k == 0), stop=(k == 7)),
                         reads=[("ht", s)] + (WAK if k == 0 else []), writes=[("ps", b)])
            for ch in range(4):
                col0 = (WA_DQ if ch < 2 else WA_DK) + (ch % 2) * 128
                ba, bb = bank.next(), bank.next()
                mm8(ba, col0)
                mm8(bb, col0 + 256)
                ts_ = ch % 2
                P.op("dve", lambda e, ba=ba, ts_=ts_, s=s: e.tensor_tensor(out=t1[ts_][:], in0=ps[:, ba, :], in1=cs[s][:, 0, :],
                                                                         op=ALU.mult),
                     reads=[("ps", ba), ("cs", s)], writes=[("t1", ts_)])
                P.op("dve", lambda e, bb=bb, ts_=ts_, s=s: e.tensor_tensor(out=t2[ts_][:], in0=ps[:, bb, :], in1=cs[s][:, 1, :],
                                                                         op=ALU.mult),
                     reads=[("ps", bb), ("cs", s)], writes=[("t2", ts_)])
                P.op("pool", lambda e, ch=ch, ts_=ts_, s=s: e.tensor_tensor(out=QKs[s][:, ch, :], in0=t1[ts_][:], in1=t2[ts_][:],
                                                                          op=ALU.add),
                     reads=[("t1", ts_), ("t2", ts_)], writes=[("QKs", s, ch)])
            for ch in range(4, 8):
                col0 = WA_SQ + (ch - 4) * 128
                b = bank.next()
                mm8(b, col0)
                P.op("act", lambda e, b=b, ch=ch, s=s: e.copy(out=QKs[s][:, ch, :], in_=ps[:, b, :]),
                     reads=[("ps", b)], writes=[("QKs", s, ch)])
            P.op("sp", lambda e, T=T, s=s: e.dma_start(out=QK_v[:, :, T * 512:(T + 1) * 512], in_=QKs[s][:]),
                 reads=[("QKs", s, ch) for ch in range(8)], sem="QKs%d" % s, ndma=1)
            for sub in range(4):
                b = bank.next()
                for k in range(8):
                    P.op("pe", lambda e, k=k, b=b, sub=sub, s=s: e.matmul(ps[:, b, :], lhsT=ht[s][:, k, sub * 128:(sub + 1) * 128],
                                                                        rhs=WA[:, k, WA_V:WA_V + 512],
                                                                        start=(k == 0), stop=(k == 7)),
                         reads=[("ht", s)], writes=[("ps", b)])
                P.op("act", lambda e, b=b, sub=sub, s=s: e.copy(out=Vs[s][:, sub, :], in_=ps[:, b, :]),
                     reads=[("ps", b)], writes=[("Vs", s, sub)])
            P.op("sp", lambda e, T=T, s=s: e.dma_start(out=V_v[:, T * 4:(T + 1) * 4, :], in_=Vs[s][:]),
                 reads=[("Vs", s, k) for k in range(4)], sem="Vs%d" % s, ndma=1)

        hm = hT_mine_d.rearrange("(c p) t -> p c t", p=128)
        gT_v = gT_d.rearrange("(c p) t -> p c t", p=128)
        gsl = Rot(2)
        if il is not None:
            selt = A.sb("selt", [128, 2], F32)
            htb = A.sb("htb", [128, 8, 512], BF16)
            P.op("sp", lambda e: e.dma_start(out=selt[:], in_=bcast_ap(il["sel"], 2)), writes=["selt"], sem="selt", ndma=1)
        gtiles = list(range(ntt) if il is None else range(il["vt0"], il["vt0"] + ntt // 2))

        def ld_gate(T):
            s = T % 2
            if il is None:
                P.op("sp", lambda e, T=T, s=s: e.dma_start(out=ht[s][:], in_=hm[:, :, T * 512:(T + 1) * 512]),
                     writes=[("ht", s)], sem="ht%d" % s, ndma=1)
            else:
                ra, rb_ = 2 * T, 2 * T + 1
                hva = hT_all_d[ra // ntt].rearrange("(c p) t -> p c t", p=128)
                hvb = hT_all_d[rb_ // ntt].rearrange("(c p) t -> p c t", p=128)
                P.op("sp", lambda e, hva=hva, ra=ra, s=s: e.dma_start(out=ht[s][:], in_=hva[:, :, (ra % ntt) * 512:(ra % ntt + 1) * 512]),
                     writes=[("ht", s)], sem="ht%d" % s, ndma=1)
                P.op("sp", lambda e, hvb=hvb, rb_=rb_: e.dma_start(out=htb[:], in_=hvb[:, :, (rb_ % ntt) * 512:(rb_ % ntt + 1) * 512]),
                     writes=["htb"], sem="htb", ndma=1)
                P.op("dve", lambda e, s=s: e.tensor_scalar(out=ht[s][:], in0=ht[s][:], scalar1=selt[:, 0:1], scalar2=None, op0=ALU.mult),
                     reads=[("ht", s), "selt"], writes=[("ht", s)])
                P.op("dve", lambda e, s=s: e.scalar_tensor_tensor(out=ht[s][:], in0=htb[:], scalar=selt[:, 1:2], in1=ht[s][:],
                                                                  op0=ALU.mult, op1=ALU.add),
                     reads=[("ht", s), "htb", "selt"], writes=[("ht", s)])

        ld_gate(gtiles[0])
        for gi_, T in enumerate(gtiles):
            s = T % 2
            if gi_ + 1 < len(gtiles):
                ld_gate(gtiles[gi_ + 1])
            for half in range(2):
                g = gsl.next()
                for c8 in range(8):
                    ch = half * 8 + c8
                    b = bank.next()
                    for k in range(8):
                        P.op("pe", lambda e, k=k, b=b, ch=ch, s=s: e.matmul(ps[:, b, :],
                                                                          lhsT=WA[:, k, WA_G + ch * 128:WA_G + (ch + 1) * 128],
                                                                          rhs=ht[s][:, k, :], start=(k == 0), stop=(k == 7)),
                             reads=[("ht", s)], writes=[("ps", b)])
                    P.op("act", lambda e, b=b, ch=ch, c8=c8, g=g: e.activation(out=Gs[g][:, c8, :], in_=ps[:, b, :],
                                                                              func=AF.Sigmoid, bias=bg[:, ch:ch + 1]),
                         reads=[("ps", b), "bg"], writes=[("Gs", g, c8)])
                P.op("sp", lambda e, T=T, half=half, g=g: e.dma_start(out=gT_v[:, half * 8:(half + 1) * 8, T * 512:(T + 1) * 512],
                                                                    in_=Gs[g][:]),
                     reads=[("Gs", g, k) for k in range(8)], sem="Gs%d" % g, ndma=1)
        P.flush()


def phase_B(P, nc, S, QK_d, V_d, lam_h, lam_off, subln_h, subln_off, lam_init, consts_d, o_send_d, il=None):
    NT = S // 2
    NCH = S // 128
    NQT = S // 512
    ntt = NT // 512
    with ExitStack() as st:
        A = Alloc(nc, st, "b_")
        cst = A.sb("cst", [128, NCONST], BF16)
        qk = [A.sb("qk%d" % i, [64, S], BF16) for i in range(4)]
        Vd = A.sb("Vd", [128, NCH, 129], BF16)
        Vsb = A.sb("Vsb", [128, NCH, 64], BF16)
        Pm = [A.sb("Pm%d" % i, [128, 1024], BF16) for i in range(3)]
        Et = [A.sb("E%d" % i, [128, 1024], BF16) for i in range(4)]
        spt = [A.sb("sp%d" % i, [128, 1024], BF16) for i in range(3)]
        cst_ = [A.sb("cs%d" % i, [128, 512], BF16) for i in range(4)]
        xct = [A.sb("xc%d" % i, [128, 1024], BF16) for i in range(2)]
        At = [A.sb("A%d" % i, [128, 1024], BF16) for i in range(3)]
        o1 = A.sb("o1", [128, 128], F32)
        oo = A.sb("oo", [128, 128], F32)
        on = A.sb("on", [128, 128], BF16)
        junk = A.sb("junk", [128, 128], BF16)
        rr = A.sb("rr", [128, 4], F32)
        ssq = A.sb("ssq", [128, 1], F32)
        rstd = A.sb("rstd", [128, 1], F32)
        lp = A.sb("lp", [128, 256], F32)
        lt = A.sb("lt", [128, 128], F32)
        lsum = A.sb("lsum", [128, 2], F32)
        nlam = A.sb("nlam", [128, 1], F32)
        sublnb = A.sb("sublnb", [128, 128], F32)
        odT = [A.sb("odT%d" % i, [128, 512], BF16) for i in range(2)]
        osT = [A.sb("osT%d" % i, [64, 512], BF16) for i in range(2)]
        ps = A.ps("ps", [128, 8, 512], F32)
        rl = A.sb("rl", [128, 2, 512], F32)
        Psm = [A.sb("Psm%d" % i, [128, 512], BF16) for i in range(3)]
        psr = Rot(3)
        o1w = A.sb("o1w", [128, 512], F32)
        o2w = A.sb("o2w", [128, 512], F32)
        oow = A.sb("oow", [128, 512], F32)
        sqw = A.sb("sqw", [128, 512], BF16)
        rrw = A.sb("rrw", [128, 512], F32)
        sublnc = A.sb("sublnc", [128, 1], F32)
        ident = cst[:, C_ID:C_ID + 128]
        tri = cst[:, C_TRI:C_TRI + 128]
        ones = cst[:, C_ONE:C_ONE + 128]

        P.op("sp", lambda e: e.dma_start(out=cst[:], in_=consts_d), writes=["consts"], sem="cst", ndma=1)
        if il is None:
            NVT = NQT
            nch = lambda qt: 4 * qt + 4
            mbase = lambda qt: 4 * qt
            mask_ap = lambda kind, m: cst[:, (C_MI if kind == "I" else C_MS) + 384 - m * 128:
                                          (C_MI if kind == "I" else C_MS) + 384 - m * 128 + 512]
            qsrc = lambda i: qk[i]
            o_dst = lambda qt, r0, r1: o_send_d[qt // ntt, r0:r1, (qt % ntt) * 512:(qt % ntt + 1) * 512]
        else:
            NVT = NQT // 2
            nch = lambda qt: 8 * qt + 8
            mbase = lambda qt: 8 * qt
            MtI = A.sb("MtI", [128, 8, 512], BF16)
            MtS = A.sb("MtS", [128, 8, 512], BF16)
            selt = A.sb("selt", [128, 2], F32)
            qsel = [A.sb("qsel%d" % i, [64, NT], BF16) for i in range(2)]
            qtmp = A.sb("qtmp", [64, NT], BF16)
            P.op("sp", lambda e: e.dma_start(out=MtI[:], in_=il["mtI"]), writes=["consts2"], sem="MtI", ndma=1)
            P.op("sp", lambda e: e.dma_start(out=MtS[:], in_=il["mtS"]), writes=["consts3"], sem="MtS", ndma=1)
            P.op("sp", lambda e: e.dma_start(out=selt[:], in_=bcast_ap(il["sel"], 2)), writes=["selt"], sem="selt", ndma=1)
            mask_ap = lambda kind, m: (MtI if kind == "I" else MtS)[:, m, :]
            qsrc = lambda i: qsel[i // 2]
            o_dst = lambda qt, r0, r1: o_send_d[r0:r1, qt * 512:(qt + 1) * 512]

            def blend_q(i):
                v = qk[i][:].rearrange("p (j two t) -> p j two t", two=2, t=512)
                P.op("dve", lambda e: e.tensor_scalar(out=qtmp[:].rearrange("p (j t) -> p j t", t=512), in0=v[:, :, 0, :],
                                                      scalar1=selt[0:64, 0:1], scalar2=None, op0=ALU.mult),
                     reads=[("qk", i), "selt"], writes=["qtmp"])
                P.op("dve", lambda e: e.scalar_tensor_tensor(out=qsel[i // 2][:].rearrange("p (j t) -> p j t", t=512),
                                                             in0=v[:, :, 1, :], scalar=selt[0:64, 1:2],
                                                             in1=qtmp[:].rearrange("p (j t) -> p j t", t=512),
                                                             op0=ALU.mult, op1=ALU.add),
                     reads=[("qk", i), "selt", "qtmp"], writes=[("qsel", i // 2)])
        P.op("sp", lambda e: e.dma_start(out=lp[:], in_=bcast_ap(lam_h, 256, lam_off)), writes=["lp"], sem="lp", ndma=1)
        P.op("sp", lambda e: e.dma_start(out=sublnb[:], in_=bcast_ap(subln_h, 128, subln_off)), writes=["sublnb"],
             sem="sublnb", ndma=1)
        P.op("dve", lambda e: e.tensor_tensor(out=lt[:].rearrange("p (a b) -> p a b", a=2),
                                              in0=lp[:].rearrange("p (a b c) -> p a b c", a=2, b=2)[:, :, 0, :],
                                              in1=lp[:].rearrange("p (a b c) -> p a b c", a=2, b=2)[:, :, 1, :],
                                              op=ALU.mult), reads=["lp"], writes=["lt"])
        P.op("dve", lambda e: e.reduce_sum(out=lsum[:], in_=lt[:].rearrange("p (a b) -> p a b", a=2),
                                           axis=mybir.AxisListType.X), reads=["lt"], writes=["lsum"])
        P.op("act", lambda e: e.activation(out=lsum[:], in_=lsum[:], func=AF.Exp), reads=["lsum"], writes=["lsum"])
        P.op("dve", lambda e: e.tensor_tensor(out=nlam[:], in0=lsum[:, 1:2], in1=lsum[:, 0:1], op=ALU.subtract),
             reads=["lsum"], writes=["nlam"])
        lcf = A.sb("lcf", [128, 2], F32)
        lc_h, lc_off = lam_init if isinstance(lam_init, tuple) else (lam_init, 0)
        P.op("sp", lambda e: e.dma_start(out=lcf[:], in_=bcast_ap(lc_h, 2, lc_off)), writes=["lcf"], sem="lcf", ndma=1)
        P.op("dve", lambda e: e.tensor_tensor(out=nlam[:], in0=nlam[:], in1=lcf[:, 0:1], op=ALU.add),
             reads=["nlam", "lcf"], writes=["nlam"])
        P.op("dve", lambda e: e.tensor_scalar(out=sublnb[:], in0=sublnb[:], scalar1=lcf[:, 1:2], scalar2=None,
                                              op0=ALU.mult), reads=["sublnb", "lcf"], writes=["sublnb"])
        P.op("sp", lambda e: e.dma_start(out=sublnc[:], in_=bass.AP(subln_h, subln_off, [[1, 128], [1, 1]])),
             writes=["sublnc"], sem="sublnc", ndma=1)
        P.op("dve", lambda e: e.tensor_tensor(out=sublnc[:], in0=sublnc[:], in1=lcf[:, 1:2], op=ALU.mult),
             reads=["sublnc", "lcf"], writes=["sublnc"])

        V_v = V_d.rearrange("(c p) d -> p c d", p=128)

        sp_slot = Rot(2)
        pmr = Rot(3)
        odr = Rot(2)
        ptr = Rot(8)

        def acc_ap(comp, qs):
            r = comp * 4 + qs
            return ps[:, 4 + r // 3, (r % 3) * 136:(r % 3) * 136 + 129]

        def acc_first(comp, qs):
            return (comp * 4 + qs) % 3 == 0

        for hd in range(2):
            for i in range(4):
                row0 = (0 if i % 2 == 0 else 256) + (hd * 2 + i // 2) * 64
                P.op("sp", lambda e, i=i, row0=row0: e.dma_start(out=qk[i][:], in_=QK_d[row0:row0 + 64, :]),
                     writes=[("qk", i)], sem="qk%d" % i, ndma=1)
            P.op("sp", lambda e, hd=hd: e.dma_start(out=Vd[:, :, 0:128], in_=V_v[:, :, hd * 128:(hd + 1) * 128]),
                 writes=["Vd"], sem="Vd", ndma=1)
            if il is not None:
                blend_q(0)
                blend_q(2)
            for qt in range(NVT):
                steps = [(comp, c0) for comp in range(2) for c0 in range(0, nch(qt), 2)]
                pend = None
                pendL = None

                def emitL(pcomp, pc0, pss, lastc):
                    P.op("pe", lambda e: e.matmul(ps[:, 6 + pcomp, :], lhsT=ones, rhs=Psm[pss][:], start=(pc0 == 0),
                                                  stop=(pc0 + 1 == lastc)),
                         reads=[("Psm", pss), "consts"], writes=[("LT", pcomp)])

                for item in steps + [None]:
                    cur = None
                    if item is not None:
                        comp, c0 = item
                        sb_ = sp_slot.next() * 2
                        pslot = pmr.next()
                        QT, KT = qsrc(2 * comp), qk[2 * comp + 1]
                        qkey = ("qk", 2 * comp) if il is None else ("qsel", comp)
                        for h in range(2):
                            c = c0 + h
                            P.op("pe", lambda e, b=sb_ + h, c=c, qt=qt, QT=QT, KT=KT: e.matmul(
                                ps[:, b, :], lhsT=KT[:, c * 128:(c + 1) * 128], rhs=QT[:, qt * 512:(qt + 1) * 512],
                                start=True, stop=True),
                                reads=[qkey, ("qk", 2 * comp + 1)], writes=[("ps", sb_ + h)])
                        P.op("act", lambda e, sb_=sb_, pslot=pslot: e.activation(
                            out=Pm[pslot][:].rearrange("p (a b) -> p a b", a=2), in_=ps[:, sb_:sb_ + 2, :], func=AF.Exp),
                            reads=[("ps", sb_), ("ps", sb_ + 1)], writes=[("Pm", pslot)])
                        for h in range(2):
                            j = c0 + h - mbase(qt)
                            if j >= 0:
                                P.op("dve", lambda e, pslot=pslot, j=j, h=h: e.tensor_tensor(
                                    out=Pm[pslot][:, h * 512:(h + 1) * 512], in0=Pm[pslot][:, h * 512:(h + 1) * 512],
                                    in1=mask_ap("I", j), op=ALU.mult),
                                    reads=[("Pm", pslot), "consts", "consts2"], writes=[("Pm", pslot)])
                        cur = (comp, c0, pslot)
                    if pend is not None:
                        pcomp, pc0, ppslot = pend
                        lastc = nch(qt) - 1
                        for h in range(2):
                            pc = pc0 + h
                            P.op("pe", lambda e, pcomp=pcomp, pc=pc, ppslot=ppslot, h=h, lastc=lastc: e.matmul(
                                ps[:, 4 + pcomp, :], lhsT=Vd[:, pc, 0:128], rhs=Pm[ppslot][:, h * 512:(h + 1) * 512],
                                start=(pc == 0), stop=(pc == lastc)),
                                reads=[("Pm", ppslot), "Vd"], writes=[("OT", pcomp)])
                        pss = psr.next()
                        P.op("dve", lambda e, ppslot=ppslot, pss=pss: e.tensor_tensor(out=Psm[pss][:], in0=Pm[ppslot][:, 0:512],
                                                                                   in1=Pm[ppslot][:, 512:1024], op=ALU.add),
                             reads=[("Pm", ppslot)], writes=[("Psm", pss)])
                        if pendL is not None:
                            emitL(*pendL)
                        pendL = (pcomp, pc0, pss, lastc)
                    pend = cur
                if pendL is not None:
                    emitL(*pendL)
                    pendL = None
                od = odr.next()
                for comp in range(2):
                    P.op("act", lambda e, comp=comp: e.activation(out=rl[:, comp, :], in_=ps[:, 6 + comp, :], func=AF.Ln),
                         reads=[("LT", comp)], writes=[("rl", comp)])
                    P.op("act", lambda e, comp=comp: e.activation(out=rl[:, comp, :], in_=rl[:, comp, :], func=AF.Exp, scale=-1.0),
                         reads=[("rl", comp)], writes=[("rl", comp)])
                P.op("dve", lambda e: e.tensor_tensor(out=o1w[:], in0=ps[:, 4, :], in1=rl[:, 0, :], op=ALU.mult),
                     reads=[("OT", 0), ("rl", 0)], writes=["o1w"])
                P.op("dve", lambda e: e.tensor_tensor(out=o2w[:], in0=ps[:, 5, :], in1=rl[:, 1, :], op=ALU.mult),
                     reads=[("OT", 1), ("rl", 1)], writes=["o2w"])
                P.op("dve", lambda e: e.scalar_tensor_tensor(out=oow[:], in0=o2w[:], scalar=nlam[:, 0:1], in1=o1w[:],
                                                             op0=ALU.mult, op1=ALU.add),
                     reads=["o1w", "o2w", "nlam"], writes=["oow"])
                P.op("act", lambda e: e.activation(out=sqw[:], in_=oow[:], func=AF.Square), reads=["oow"], writes=["sqw"])
                P.op("pe", lambda e: e.matmul(ps[:, 6, :], lhsT=ones, rhs=sqw[:], start=True, stop=True),
                     reads=["sqw", "consts"], writes=[("LT", 0)])
                P.op("act", lambda e: e.activation(out=rrw[:], in_=ps[:, 6, :], func=AF.Ln, scale=1.0 / 128, bias=EPS),
                     reads=[("LT", 0)], writes=["rrw"])
                P.op("act", lambda e: e.activation(out=rrw[:], in_=rrw[:], func=AF.Exp, scale=-0.5),
                     reads=["rrw"], writes=["rrw"])
                P.op("dve", lambda e, od=od: e.scalar_tensor_tensor(out=odT[od][:], in0=oow[:], scalar=sublnc[:, 0:1], in1=rrw[:],
                                                                    op0=ALU.mult, op1=ALU.mult),
                     reads=["oow", "rrw", "sublnc"], writes=[("odT", od)])
                P.op("sp", lambda e, od=od, hd=hd, qt=qt: e.dma_start(
                    out=o_dst(qt, hd * 128, (hd + 1) * 128), in_=odT[od][:]),
                    reads=[("odT", od)], sem="odT%d" % od, ndma=1)

        zslot = Rot(2)
        er, spr, csr, xr, ar, osr = Rot(4), Rot(3), Rot(4), Rot(2), Rot(3), Rot(2)
        QT, KT = qk[0], qk[1]
        for hs in range(4):
            P.op("sp", lambda e, hs=hs: e.dma_start(out=QT[:], in_=QK_d[512 + hs * 64:512 + (hs + 1) * 64, :]),
                 writes=[("qk", 0)], sem="qk0", ndma=1)
            P.op("sp", lambda e, hs=hs: e.dma_start(out=KT[:], in_=QK_d[768 + hs * 64:768 + (hs + 1) * 64, :]),
                 writes=[("qk", 1)], sem="qk1", ndma=1)
            P.op("sp", lambda e, hs=hs: e.dma_start(out=Vsb[:], in_=V_v[:, :, 256 + hs * 64:256 + (hs + 1) * 64]),
                 writes=["Vsb"], sem="Vsb", ndma=1)
            if il is not None:
                blend_q(0)
            QS = qsrc(0)
            qkey = ("qk", 0) if il is None else ("qsel", 0)
            steps = []
            for qt in range(NVT):
                for c1 in range(nch(qt) - 1, 0, -2):
                    steps.append((qt, c1))
            n = len(steps)
            info = [None] * n
            cs_cur = None
            for t in range(n + 3):
                if t < n:
                    qt, c1 = steps[t]
                    zb, es = zslot.next() * 2, er.next()
                    for h in range(2):
                        c = c1 - h
                        P.op("pe", lambda e, b=zb + h, c=c, qt=qt: e.matmul(
                            ps[:, b, :], lhsT=KT[:, c * 128:(c + 1) * 128],
                            rhs=QS[:, qt * 512:(qt + 1) * 512], start=True, stop=True),
                             reads=[qkey, ("qk", 1)], writes=[("ps", zb + h)])
                    P.op("act", lambda e, zb=zb, es=es: e.activation(out=Et[es][:].rearrange("p (a b) -> p a b", a=2),
                                                                     in_=ps[:, zb:zb + 2, :], func=AF.Exp),
                         reads=[("ps", zb), ("ps", zb + 1)], writes=[("E", es)])
                    for h in range(2):
                        j = c1 - h - mbase(qt)
                        if j >= 0:
                            P.op("dve", lambda e, es=es, j=j, h=h: e.tensor_tensor(
                                out=Et[es][:, h * 512:(h + 1) * 512], in0=Et[es][:, h * 512:(h + 1) * 512],
                                in1=mask_ap("S", j), op=ALU.mult),
                                reads=[("E", es), "consts", "consts3"], writes=[("E", es)])
                    info[t] = dict(qt=qt, c1=c1, es=es)
                if 1 <= t <= n:
                    d = info[t - 1]
                    ss = spr.next()
                    P.op("act", lambda e, es=d["es"], ss=ss: e.activation(out=spt[ss][:], in_=Et[es][:], func=AF.Ln, bias=1.0),
                         reads=[("E", d["es"])], writes=[("sp", ss)])
                    d["ss"] = ss
                if 2 <= t <= n + 1:
                    d = info[t - 2]
                    qt, c1, ss = d["qt"], d["c1"], d["ss"]
                    first = (c1 == nch(qt) - 1)
                    xs = xr.next()
                    rd = [("sp", ss), "consts"] + ([] if first else [("csum", cs_cur)])
                    P.op("pe", lambda e, ss=ss, first=first: e.matmul(ps[:, 4, :], lhsT=tri, rhs=spt[ss][:, 0:512],
                                                                      start=True, stop=first), reads=rd, writes=[("ps", 4)])
                    if not first:
                        P.op("pe", lambda e, cc=cs_cur: e.matmul(ps[:, 4, :], lhsT=ones, rhs=cst_[cc][:], start=False, stop=True),
                             reads=rd, writes=[("ps", 4)])
                    P.op("pe", lambda e, ss=ss: e.matmul(ps[:, 5, :], lhsT=tri, rhs=spt[ss][:, 512:1024], start=True, stop=False),
                         reads=rd, writes=[("ps", 5)])
                    if first:
                        P.op("pe", lambda e, ss=ss: e.matmul(ps[:, 5, :], lhsT=ones, rhs=spt[ss][:, 0:512], start=False, stop=True),
                             reads=rd, writes=[("ps", 5)])
                        cm = None
                    else:
                        cm = csr.next()
                        P.op("dve", lambda e, cm=cm, ss=ss, cc=cs_cur: e.tensor_tensor(out=cst_[cm][:], in0=cst_[cc][:],
                                                                                      in1=spt[ss][:, 0:512], op=ALU.add),
                             reads=[("sp", ss), ("csum", cs_cur)], writes=[("csum", cm)])
                        P.op("pe", lambda e, cm=cm: e.matmul(ps[:, 5, :], lhsT=ones, rhs=cst_[cm][:], start=False, stop=True),
                             reads=[("csum", cm)], writes=[("ps", 5)])
                    if c1 > 1:
                        cn = csr.next()
                        if first:
                            P.op("pool", lambda e, cn=cn, ss=ss: e.tensor_tensor(out=cst_[cn][:], in0=spt[ss][:, 0:512],
                                                                                in1=spt[ss][:, 512:1024], op=ALU.add),
                                 reads=[("sp", ss)], writes=[("csum", cn)])
                        else:
                            P.op("pool", lambda e, cn=cn, cm=cm, ss=ss: e.tensor_tensor(out=cst_[cn][:], in0=cst_[cm][:],
                                                                                       in1=spt[ss][:, 512:1024], op=ALU.add),
                                 reads=[("sp", ss), ("csum", cm)], writes=[("csum", cn)])
                        cs_cur = cn
                    P.op("act", lambda e, xs=xs: e.activation(out=xct[xs][:].rearrange("p (a b) -> p a b", a=2), in_=ps[:, 4:6, :],
                                                              func=AF.Exp, scale=-1.0),
                         reads=[("ps", 4), ("ps", 5)], writes=[("xc", xs)])
                    d["xs"] = xs
                if t >= 3:
                    d = info[t - 3]
                    qt, c1, es, xs = d["qt"], d["c1"], d["es"], d["xs"]
                    first = (c1 == nch(qt) - 1)
                    as_ = ar.next()
                    P.op("dve", lambda e, as_=as_, xs=xs, es=es: e.tensor_tensor(out=At[as_][:], in0=Et[es][:], in1=xct[xs][:],
                                                                                op=ALU.mult),
                         reads=[("E", es), ("xc", xs)], writes=[("A", as_)])
                    for h in range(2):
                        c = c1 - h
                        P.op("pe", lambda e, c=c, as_=as_, h=h, first=first: e.matmul(
                            ps[0:64, 6, :], lhsT=Vsb[:, c, :], rhs=At[as_][:, h * 512:(h + 1) * 512],
                            start=(first and h == 0), stop=(c == 0)),
                            reads=[("A", as_), "Vsb"], writes=[("ps", 6)])
                    if c1 == 1:
                        os_ = osr.next()
                        P.op("act", lambda e, os_=os_: e.copy(out=osT[os_][:], in_=ps[0:64, 6, :]),
                             reads=[("ps", 6)], writes=[("osT", os_)])
                        P.op("sp", lambda e, os_=os_, hs=hs, qt=qt: e.dma_start(
                            out=o_dst(qt, 256 + hs * 64, 256 + (hs + 1) * 64),
                            in_=osT[os_][:]), reads=[("osT", os_)], sem="osT%d" % os_, ndma=1)
        P.flush()


def phase_C0(P, nc, pairs):
    with ExitStack() as st:
        A = Alloc(nc, st, "c0_")
        stg = [A.sb("stg%d" % i, [128, 4096], F32) for i in range(3)]
        cb = [A.sb("cb%d" % i, [128, 4096], BF16) for i in range(3)]
        r = Rot(3)
        ce = Rot(2)
        work = []
        for (src_d, dst_d, R, C) in pairs:
            nk = R // 128
            if C <= 4096:
                kg = max(1, 4096 // C)
                items = [(k0, min(kg, nk - k0), 0, C) for k0 in range(0, nk, kg)]
            else:
                half = C // 2
                assert half <= 4096
                items = [(k0, 1, c0, half) for k0 in range(nk) for c0 in (0, half)]
            sv = src_d.rearrange("(k p) c -> p k c", p=128)
            dv = dst_d.rearrange("(k p) c -> p k c", p=128)
            for (k0, nkk, c0, cw) in items:
                work.append((sv, dv, k0, nkk, c0, cw))
        slots = [r.next() for _ in work]

        def c0_load(i):
            sv, dv, k0, nkk, c0, cw = work[i]
            s = slots[i]
            sview = stg[s][:, 0:nkk * cw].rearrange("p (k c) -> p k c", k=nkk)
            P.op("sp", lambda e: e.dma_start(out=sview, in_=sv[:, k0:k0 + nkk, c0:c0 + cw]),
                 writes=[("stg", s)], sem="c0stg%d" % s, ndma=1)

        for i in range(min(2, len(work))):
            c0_load(i)
        for i, (sv, dv, k0, nkk, c0, cw) in enumerate(work):
            s = slots[i]
            n = nkk * cw
            cview = cb[s][:, 0:n].rearrange("p (k c) -> p k c", k=nkk)
            eng = ("dve", "pool")[ce.next()]
            P.op(eng, lambda e, s=s, n=n: e.tensor_copy(out=cb[s][:, 0:n], in_=stg[s][:, 0:n]),
                 reads=[("stg", s)], writes=[("cb", s)])
            if i + 2 < len(work):
                c0_load(i + 2)
            P.op("sp", lambda e, cview=cview, k0=k0, nkk=nkk, c0=c0, cw=cw, dv=dv: e.dma_start(
                out=dv[:, k0:k0 + nkk, c0:c0 + cw], in_=cview), reads=[("cb", s)], sem="c0cb%d" % s, ndma=1)
        P.flush()


def phase_C1(P, nc, NT, o_recv_d, gT_d, x_d, WCO_d, WCOUT_d, WCI_d, WCFO_d, gffn_h, gffn_off, gfin_h, consts_d,
             xo_d, final, il=None):
    with ExitStack() as st:
        A = Alloc(nc, st, "c1_")
        cst = A.sb("cst", [128, NCONST], BF16)
        w_o = A.sb("w_o", [128, 8, 1024], BF16)
        w_out = A.sb("w_out", [128, 8, 1024], BF16)
        w_fo = A.sb("w_fo", [128, NF, 1024], BF16)
        gfb = A.sb("gfb", [128, 1024], F32)
        gnb = A.sb("gnb", [128, 1024], F32) if final else None
        odT = A.sb("odT", [128, 4, 512], BF16)
        osT = A.sb("osT", [128, 4, 512], BF16)
        gd = [A.sb("gd%d" % i, [128, 2, 512], BF16) for i in range(2)]
        m1 = [A.sb("m1_%d" % i, [128, 512], F32) for i in range(2)]
        m2 = [A.sb("m2_%d" % i, [128, 512], F32) for i in range(2)]
        mT = A.sb("mT", [128, 8, 512], BF16)
        xm = A.sb("xm", [128, 4, 1024], F32)
        junk = A.sb("junk", [128, 1024], BF16)
        ssq = A.sb("ssq", [128, 1], F32)
        rstd = A.sb("rstd", [128, 1], F32)
        hb = A.sb("hb", [128, 1024], BF16)
        h2T = A.sb("h2T", [128, 8, 512], BF16)
        aT = A.sb("aT", [128, NF, 512], BF16)
        sg = [A.sb("sg%d" % i, [128, 512], F32) for i in range(2)]
        wi = [A.sb("wi%d" % i, [128, 2, 8, 256], BF16) for i in range(2)]
        xo = [A.sb("xo%d" % i, [128, 1024], F32) for i in range(2)] if final else None
        ps = A.ps("ps", [128, 7, 512], F32)
        psT = A.ps("psT", [128, 1024], BF16)
        bank = Rot(7)
        ident = cst[:, C_ID:C_ID + 128]
        if il is not None:
            selt = A.sb("selt", [128, 2], F32)
            xb = A.sb("xb", [128, 2, 1024], F32)
            gd2 = [A.sb("gd2_%d" % i, [128, 2, 512], BF16) for i in range(2)]
            P.op("sp", lambda e: e.dma_start(out=selt[:], in_=bcast_ap(il["sel"], 2)), writes=["selt"], sem="selt", ndma=1)

        P.op("sp", lambda e: e.dma_start(out=cst[:], in_=consts_d), writes=["consts"], sem="cst", ndma=1)
        P.op("sp", lambda e: e.dma_start(out=w_o[:], in_=WCO_d.rearrange("(k p) c -> p k c", p=128)), writes=["w_o"],
             sem="w_o", ndma=1)
        P.op("sp", lambda e: e.dma_start(out=w_out[:], in_=WCOUT_d.rearrange("(k p) c -> p k c", p=128)), writes=["w_out"],
             sem="w_out", ndma=1)
        P.op("sp", lambda e: e.dma_start(out=w_fo[:], in_=WCFO_d.rearrange("(k p) c -> p k c", p=128)), writes=["w_fo"],
             sem="w_fo", ndma=1)
        P.op("sp", lambda e: e.dma_start(out=gfb[:], in_=bcast_ap(gffn_h, 1024, gffn_off)), writes=["gb"], sem="gfb", ndma=1)
        if final:
            P.op("sp", lambda e: e.dma_start(out=gnb[:], in_=bcast_ap(gfin_h, 1024)), writes=["gnb"], sem="gnb", ndma=1)
        WI_v = WCI_d.rearrange("(k p) c -> p k c", p=128)
        if il is None:
            gT_v = gT_d.rearrange("(h c p) t -> p h c t", h=2, p=128)
        else:
            gT_vs = [gT_d[th].rearrange("(h c p) t -> p h c t", h=2, p=128) for th in range(2)]
        ntt = NT // 512
        wir = Rot(2)
        def ld_o(t0):
            P.op("sp", lambda e, t0=t0: [e.dma_start(out=odT[:, 2 * r:2 * r + 2, :],
                                                     in_=o_recv_d[r, 0:256, t0:t0 + 512].rearrange("(c p) t -> p c t", p=128))
                                         for r in range(2)], writes=["odT"], sem="odT", ndma=2)
            P.op("sp", lambda e, t0=t0: [e.dma_start(out=osT[:, 2 * r:2 * r + 2, :],
                                                     in_=o_recv_d[r, 256:512, t0:t0 + 512].rearrange("(c p) t -> p c t", p=128))
                                         for r in range(2)], writes=["osT"], sem="osT", ndma=2)

        ld_o(0)
        for tb in range(NT // 512):
            t0 = tb * 512
            xmk = [("xm", ts, ch) for ts in range(4) for ch in range(2)]
            if il is None:
                P.op("sp", lambda e, t0=t0: e.dma_start(out=xm[:], in_=x_d[t0:t0 + 512, :].rearrange("(s p) c -> p s c", p=128)),
                     writes=xmk, sem="xm", ndma=1)
            else:
                ra, rb = 2 * tb * 512, (2 * tb + 1) * 512
                P.op("sp", lambda e, ra=ra: e.dma_start(out=xm[:], in_=x_d[ra:ra + 512, :].rearrange("(s p) c -> p s c", p=128)),
                     writes=xmk, sem="xm", ndma=1)
                P.op("dve", lambda e: e.tensor_scalar(out=xm[:], in0=xm[:], scalar1=selt[:, 0:1], scalar2=None, op0=ALU.mult),
                     reads=xmk + ["selt"], writes=xmk)
                for hb_ in range(2):
                    P.op("sp", lambda e, rb=rb, hb_=hb_: e.dma_start(
                        out=xb[:], in_=x_d[rb + hb_ * 256:rb + (hb_ + 1) * 256, :].rearrange("(s p) c -> p s c", p=128)),
                        writes=["xb"], sem="xb", ndma=1)
                    kk = [("xm", ts, ch) for ts in range(2 * hb_, 2 * hb_ + 2) for ch in range(2)]
                    P.op("dve", lambda e, hb_=hb_: e.scalar_tensor_tensor(
                        out=xm[:, 2 * hb_:2 * hb_ + 2, :], in0=xb[:], scalar=selt[:, 1:2], in1=xm[:, 2 * hb_:2 * hb_ + 2, :],
                        op0=ALU.mult, op1=ALU.add), reads=["xb", "selt"] + kk, writes=kk)
            for j in range(8):
                s = j % 2
                if il is None:
                    P.op("sp", lambda e, j=j, s=s, t0=t0: e.dma_start(out=gd[s][:], in_=gT_v[:, :, j, t0:t0 + 512]),
                         writes=[("gd", s)], sem="gd%d" % s, ndma=1)
                else:
                    th_, tt0 = (2 * tb) // ntt, (2 * tb) % ntt
                    gv = gT_vs[th_]
                    P.op("sp", lambda e, j=j, s=s, gv=gv, tt0=tt0: e.dma_start(out=gd[s][:], in_=gv[:, :, j, tt0 * 512:(tt0 + 1) * 512]),
                         writes=[("gd", s)], sem="gd%d" % s, ndma=1)
                    P.op("sp", lambda e, j=j, s=s, gv=gv, tt0=tt0: e.dma_start(out=gd2[s][:],
                                                                              in_=gv[:, :, j, (tt0 + 1) * 512:(tt0 + 2) * 512]),
                         writes=[("gd2", s)], sem="gd2_%d" % s, ndma=1)
                    P.op("pool", lambda e, s=s: e.tensor_scalar(out=gd[s][:], in0=gd[s][:], scalar1=selt[:, 0:1], scalar2=None,
                                                                op0=ALU.mult), reads=[("gd", s), "selt"], writes=[("gd", s)])
                    P.op("dve", lambda e, s=s: e.scalar_tensor_tensor(out=gd[s][:], in0=gd2[s][:], scalar=selt[:, 1:2], in1=gd[s][:],
                                                                      op0=ALU.mult, op1=ALU.add),
                         reads=[("gd", s), ("gd2", s), "selt"], writes=[("gd", s)])
                b1, b2 = bank.next(), bank.next()
                for kc in range(4):
                    P.op("pe", lambda e, kc=kc, b1=b1, j=j: e.matmul(ps[:, b1, :], lhsT=w_o[:, kc, j * 128:(j + 1) * 128],
                                                                     rhs=odT[:, kc, :], start=(kc == 0), stop=(kc == 3)),
                         reads=["w_o", "odT"], writes=[("ps", b1)])
                for kc in range(4):
                    P.op("pe", lambda e, kc=kc, b2=b2, j=j: e.matmul(ps[:, b2, :], lhsT=w_o[:, 4 + kc, j * 128:(j + 1) * 128],
                                                                     rhs=osT[:, kc, :], start=(kc == 0), stop=(kc == 3)),
                         reads=["w_o", "osT"], writes=[("ps", b2)])
                P.op("dve", lambda e, b1=b1, s=s: e.tensor_tensor(out=m1[s][:], in0=ps[:, b1, :], in1=gd[s][:, 0, :], op=ALU.mult),
                     reads=[("ps", b1), ("gd", s)], writes=[("m1", s)])
                P.op("dve", lambda e, b2=b2, s=s: e.tensor_tensor(out=m2[s][:], in0=ps[:, b2, :], in1=gd[s][:, 1, :], op=ALU.mult),
                     reads=[("ps", b2), ("gd", s)], writes=[("m2", s)])
                P.op("pool", lambda e, j=j, s=s: e.tensor_tensor(out=mT[:, j, :], in0=m1[s][:], in1=m2[s][:], op=ALU.add),
                     reads=[("m1", s), ("m2", s)], writes=[("mT", j)])
            if tb + 1 < NT // 512:
                ld_o(t0 + 512)
            for ts in range(4):
                for ch in range(2):
                    b = bank.next()
                    for k in range(8):
                        P.op("pe", lambda e, k=k, b=b, ts=ts, ch=ch: e.matmul(
                            ps[:, b, :], lhsT=mT[:, k, ts * 128:(ts + 1) * 128], rhs=w_out[:, k, ch * 512:(ch + 1) * 512],
                            start=(k == 0), stop=(k == 7)),
                            reads=[("mT", k), "w_out"], writes=[("ps", b)])
                    P.op("dve", lambda e, b=b, ts=ts, ch=ch: e.tensor_tensor(
                        out=xm[:, ts, ch * 512:(ch + 1) * 512], in0=ps[:, b, :], in1=xm[:, ts, ch * 512:(ch + 1) * 512], op=ALU.add),
                        reads=[("ps", b), ("xm", ts, ch)], writes=[("xm", ts, ch)])
            for ts in range(4):
                emit_norm_transpose2(P, [("xm", ts, 0), ("xm", ts, 1)], xm[:, ts, :], gfb, junk, ssq, rstd, hb, psT, ident,
                                     h2T[:, :, ts * 128:(ts + 1) * 128], [("h2T", ts)])
            for g in range(NF // 2):
                w = wir.next()
                f0 = g * 2
                P.op("sp", lambda e, w=w, f0=f0: [e.dma_start(out=wi[w][:, 0, :, :], in_=WI_v[:, :, f0 * 128:f0 * 128 + 256]),
                                                  e.dma_start(out=wi[w][:, 1, :, :],
                                                              in_=WI_v[:, :, FFN + f0 * 128:FFN + f0 * 128 + 256])],
                     writes=[("wi", w)], sem="wi%d" % w, ndma=2)
                for fl in range(2):
                    f = f0 + fl
                    bg_, bu_ = bank.next(), bank.next()
                    for (bb_, gu) in ((bg_, 0), (bu_, 1)):
                        for k in range(8):
                            P.op("pe", lambda e, k=k, bb_=bb_, gu=gu, fl=fl, w=w: e.matmul(
                                ps[:, bb_, :], lhsT=wi[w][:, gu, k, fl * 128:(fl + 1) * 128], rhs=h2T[:, k, :],
                                start=(k == 0), stop=(k == 7)),
                                reads=[("wi", w)] + [("h2T", ts) for ts in range(4)], writes=[("ps", bb_)])
                    s = f % 2
                    P.op("act", lambda e, bg_=bg_, s=s: e.activation(out=sg[s][:], in_=ps[:, bg_, :], func=AF.Silu),
                         reads=[("ps", bg_)], writes=[("sg", s)])
                    P.op("dve", lambda e, bu_=bu_, s=s, f=f: e.tensor_tensor(out=aT[:, f, :], in0=ps[:, bu_, :], in1=sg[s][:],
                                                                             op=ALU.mult),
                         reads=[("ps", bu_), ("sg", s)], writes=[("aT", f)])
            for ts in range(4):
                for ch in range(2):
                    b = bank.next()
                    for f in range(NF):
                        P.op("pe", lambda e, f=f, b=b, ts=ts, ch=ch: e.matmul(
                            ps[:, b, :], lhsT=aT[:, f, ts * 128:(ts + 1) * 128], rhs=w_fo[:, f, ch * 512:(ch + 1) * 512],
                            start=(f == 0), stop=(f == NF - 1)),
                            reads=[("aT", f), "w_fo"], writes=[("ps", b)])
                    P.op("dve", lambda e, b=b, ts=ts, ch=ch: e.tensor_tensor(
                        out=xm[:, ts, ch * 512:(ch + 1) * 512], in0=ps[:, b, :], in1=xm[:, ts, ch * 512:(ch + 1) * 512], op=ALU.add),
                        reads=[("ps", b), ("xm", ts, ch)], writes=[("xm", ts, ch)])
            if not final:
                P.op("sp", lambda e, t0=t0: e.dma_start(out=xo_d[t0:t0 + 512, :].rearrange("(s p) c -> p s c", p=128), in_=xm[:]),
                     reads=[("xm", ts, ch) for ts in range(4) for ch in range(2)], sem="xm", ndma=1)
            else:
                for ts in range(4):
                    xs = ts % 2
                    xk = [("xm", ts, 0), ("xm", ts, 1)]
                    P.op("act", lambda e, ts=ts: e.activation(out=junk[:], in_=xm[:, ts, :], func=AF.Square, accum_out=ssq[:]),
                         reads=xk, writes=["junk", "ssq"])
                    P.op("act", lambda e: e.activation(out=rstd[:], in_=ssq[:], func=AF.Ln, scale=1.0 / D, bias=EPS),
                         reads=["ssq"], writes=["rstd"])
                    P.op("act", lambda e: e.activation(out=rstd[:], in_=rstd[:], func=AF.Exp, scale=-0.5),
                         reads=["rstd"], writes=["rstd"])
                    P.op("dve", lambda e, ts=ts, xs=xs: e.scalar_tensor_tensor(out=xo[xs][:], in0=xm[:, ts, :], scalar=rstd[:, 0:1],
                                                                               in1=gnb[:], op0=ALU.mult, op1=ALU.mult),
                         reads=xk + ["rstd", "gnb"], writes=[("xo", xs)])
                    P.op("sp", lambda e, ts=ts, xs=xs, t0=t0: e.dma_start(out=xo_d[t0 + ts * 128:t0 + (ts + 1) * 128, :], in_=xo[xs][:]),
                         reads=[("xo", xs)], sem="xo%d" % xs, ndma=1)
        P.flush()


def emit_norm_transpose2(P, xkeys, x_ap, gb, junk, ssq, rstd, hb, psT, ident, out_ap, out_keys):
    P.op("act", lambda e: e.activation(out=junk[:], in_=x_ap, func=AF.Square, accum_out=ssq[:]),
         reads=xkeys, writes=["junk", "ssq"])
    P.op("act", lambda e: e.activation(out=rstd[:], in_=ssq[:], func=AF.Ln, scale=1.0 / D, bias=EPS),
         reads=["ssq"], writes=["rstd"])
    P.op("act", lambda e: e.activation(out=rstd[:], in_=rstd[:], func=AF.Exp, scale=-0.5),
         reads=["rstd"], writes=["rstd"])
    P.op("dve", lambda e: e.scalar_tensor_tensor(out=hb[:], in0=x_ap, scalar=rstd[:, 0:1], in1=gb[:],
                                                 op0=ALU.mult, op1=ALU.mult),
         reads=list(xkeys) + ["rstd", "gb"], writes=["hb"])
    for j in range(8):
        P.op("pe", lambda e, j=j: e.transpose(out=psT[:, j * 128:(j + 1) * 128], in_=hb[:, j * 128:(j + 1) * 128],
                                             identity=ident),
             reads=["hb", "consts"], writes=["psT"])
    P.op("act", lambda e: e.copy(out=out_ap, in_=psT[:].rearrange("p (c t) -> p c t", c=8)),
         reads=["psT"], writes=out_keys)


def _nc():
    return bass.Bass("TRN2", target_bir_lowering=False)


def build_LA(S):
    NT = S // 2
    nc = _nc()
    x = nc.dram_tensor("x", [NT, D], F32, kind="ExternalInput")
    g = nc.dram_tensor("g", [D], F32, kind="ExternalInput")
    consts = nc.dram_tensor("consts", [128, NCONST], BF16, kind="ExternalInput")
    hT = nc.dram_tensor("hT", [D, NT], BF16, kind="ExternalOutput")
    with ExitStack() as st:
        P = Prog(nc, st)
        phase_A1(P, nc, NT, x.ap(), g, consts.ap(), hT.ap())
    return nc


def build_LB(S):
    NT = S // 2
    nc = _nc()
    hT_all = nc.dram_tensor("hT_all", [2, D, NT], BF16, kind="ExternalInput")
    hT_mine = nc.dram_tensor("hT_mine", [D, NT], BF16, kind="ExternalInput")
    wqk = nc.dram_tensor("wqk", [D, 1024], F32, kind="ExternalInput")
    wv = nc.dram_tensor("wv", [D, 512], F32, kind="ExternalInput")
    wg = nc.dram_tensor("wg", [D, 2048], F32, kind="ExternalInput")
    bg = nc.dram_tensor("bg", [128, 16], F32, kind="ExternalInput")
    cos = nc.dram_tensor("cos", [128, S], F32, kind="ExternalInput")
    sin = nc.dram_tensor("sin", [128, S], F32, kind="ExternalInput")
    lam = nc.dram_tensor("lam", [256], F32, kind="ExternalInput")
    subln = nc.dram_tensor("subln", [128], F32, kind="ExternalInput")
    lcfg = nc.dram_tensor("lcfg", [2], F32, kind="ExternalInput")
    consts = nc.dram_tensor("consts", [128, NCONST], BF16, kind="ExternalInput")
    QK = nc.dram_tensor("QK", [1024, S], BF16, kind="Internal")
    V = nc.dram_tensor("V", [S, 512], BF16, kind="Internal")
    o_send = nc.dram_tensor("o_send", [2, 512, NT], BF16, kind="ExternalOutput")
    gT = nc.dram_tensor("gT", [2048, NT], BF16, kind="ExternalOutput")
    with ExitStack() as st:
        P = Prog(nc, st)
        phase_A2(P, nc, S, hT_all.ap(), hT_mine.ap(), wqk.ap(), wv.ap(), wg.ap(), bg.ap(), cos.ap(), sin.ap(),
                 QK.ap(), V.ap(), gT.ap())
        phase_B(P, nc, S, QK.ap(), V.ap(), lam, 0, subln, 0, lcfg, consts.ap(), o_send.ap())
    return nc


def build_LC(S, final):
    NT = S // 2
    nc = _nc()
    o_recv = nc.dram_tensor("o_recv", [2, 512, NT], BF16, kind="ExternalInput")
    gT = nc.dram_tensor("gT", [2048, NT], BF16, kind="ExternalInput")
    x = nc.dram_tensor("x", [NT, D], F32, kind="ExternalInput")
    wo = nc.dram_tensor("wo", [1024, D], F32, kind="ExternalInput")
    wout = nc.dram_tensor("wout", [D, D], F32, kind="ExternalInput")
    wfi = nc.dram_tensor("wfi", [D, 2 * FFN], F32, kind="ExternalInput")
    wfo = nc.dram_tensor("wfo", [FFN, D], F32, kind="ExternalInput")
    gffn = nc.dram_tensor("gffn", [D], F32, kind="ExternalInput")
    gfin = nc.dram_tensor("gfin", [D], F32, kind="ExternalInput")
    consts = nc.dram_tensor("consts", [128, NCONST], BF16, kind="ExternalInput")
    WCO = nc.dram_tensor("WCO", [1024, D], BF16, kind="Internal")
    WCOUT = nc.dram_tensor("WCOUT", [D, D], BF16, kind="Internal")
    WCI = nc.dram_tensor("WCI", [D, 2 * FFN], BF16, kind="Internal")
    WCFO = nc.dram_tensor("WCFO", [FFN, D], BF16, kind="Internal")
    xo = nc.dram_tensor("xo", [NT, D], F32, kind="ExternalOutput")
    with ExitStack() as st:
        P = Prog(nc, st)
        phase_C0(P, nc, [(wo.ap(), WCO.ap(), 1024, D), (wout.ap(), WCOUT.ap(), D, D), (wfi.ap(), WCI.ap(), D, 2 * FFN),
                         (wfo.ap(), WCFO.ap(), FFN, D)])
        phase_C1(P, nc, NT, o_recv.ap(), gT.ap(), x.ap(), WCO.ap(), WCOUT.ap(), WCI.ap(), WCFO.ap(), gffn, 0, gfin,
                 consts.ap(), xo.ap(), final)
    return nc


def _run(nc, in_maps):
    res = run_bass_kernel_spmd(nc, in_maps, core_ids=list(range(N_CORES)))
    return res.results


def _f32(a):
    return np.ascontiguousarray(np.asarray(a, dtype=np.float32))


def slice_weights(inp, l, hf):
    w_in = inp["w_in"][l]
    dq = w_in[:, hf * 256:(hf + 1) * 256]
    dk = w_in[:, 512 + hf * 256:512 + (hf + 1) * 256]
    dv = w_in[:, 1024 + hf * 256:1024 + (hf + 1) * 256]
    sq = w_in[:, 1536 + hf * 256:1536 + (hf + 1) * 256]
    sk = w_in[:, 2048 + hf * 256:2048 + (hf + 1) * 256]
    sv = w_in[:, 2560 + hf * 256:2560 + (hf + 1) * 256]
    return dict(wqk=_f32(np.concatenate([dq, dk, sq, sk], axis=1)), wv=_f32(np.concatenate([dv, sv], axis=1)),
                wg=_f32(w_in[:, 3072:5120]))


def kernel_unfused(inp, S):
    NT = S // 2
    B = inp["x"].shape[0]
    assert 2 * B == N_CORES
    consts = host_consts()
    cosT, sinT = rope_tables(S)
    xs = [_f32(inp["x"][c // 2, (c % 2) * NT:(c % 2 + 1) * NT]) for c in range(N_CORES)]
    ncA = build_LA(S)
    ncB = build_LB(S)
    for l in range(2):
        lam_init = 0.8 - 0.6 * math.exp(-0.3 * l)
        resA = _run(ncA, [dict(x=xs[c], g=_f32(inp["norm_attn"][l]), consts=consts) for c in range(N_CORES)])
        hT = [resA[c]["hT"] for c in range(N_CORES)]
        mapsB = []
        for c in range(N_CORES):
            b, hf = c // 2, c % 2
            m = slice_weights(inp, l, hf)
            m.update(hT_all=np.stack([hT[2 * b], hT[2 * b + 1]]), hT_mine=hT[c],
                     bg=_f32(np.asarray(inp["b_gate"][l]).reshape(16, 128).T),
                     cos=cosT, sin=sinT, lam=_f32(np.asarray(inp["diff_lambda"][l]).reshape(256)),
                     subln=_f32(inp["diff_subln"][l]), lcfg=np.array([-lam_init, 1.0 - lam_init], np.float32),
                     consts=consts)
            mapsB.append(m)
        resB = _run(ncB, mapsB)
        ncC = build_LC(S, final=(l == 1))
        mapsC = []
        for c in range(N_CORES):
            b, hf = c // 2, c % 2
            o_recv = np.stack([resB[2 * b + r]["o_send"][hf] for r in range(2)])
            mapsC.append(dict(o_recv=np.ascontiguousarray(o_recv), gT=resB[c]["gT"], x=xs[c],
                              wo=_f32(np.concatenate([inp["w_o_diff"][l], inp["w_o_sb"][l]], axis=0)),
                              wout=_f32(inp["w_out"][l]), wfi=_f32(inp["w_ffn_in"][l]), wfo=_f32(inp["w_ffn_out"][l]),
                              gffn=_f32(inp["norm_ffn"][l]), gfin=_f32(inp["norm_final"]), consts=consts))
        resC = _run(ncC, mapsC)
        xs = [resC[c]["xo"] for c in range(N_CORES)]
    out = np.zeros((B, S, D), np.float32)
    for c in range(N_CORES):
        out[c // 2, (c % 2) * NT:(c % 2 + 1) * NT] = xs[c]
    return out


def phase_XSEL(P, nc, NT, x_d, sel_h, xs_d):
    with ExitStack() as st:
        A = Alloc(nc, st, "xs_")
        selt = A.sb("selt", [128, 2], F32)
        xa = [A.sb("xa%d" % i, [128, 4, 1024], F32) for i in range(2)]
        xb = [A.sb("xb%d" % i, [128, 4, 1024], F32) for i in range(2)]
        P.op("sp", lambda e: e.dma_start(out=selt[:], in_=bcast_ap(sel_h, 2)), writes=["selt"], sem="selt", ndma=1)
        for j in range(NT // 512):
            s = j % 2
            ra, rb = 2 * j * 512, (2 * j + 1) * 512
            P.op("sp", lambda e, ra=ra, s=s: e.dma_start(out=xa[s][:], in_=x_d[ra:ra + 512, :].rearrange("(s p) c -> p s c", p=128)),
                 writes=[("xa", s)], sem="xa%d" % s, ndma=1)
            P.op("sp", lambda e, rb=rb, s=s: e.dma_start(out=xb[s][:], in_=x_d[rb:rb + 512, :].rearrange("(s p) c -> p s c", p=128)),
                 writes=[("xb", s)], sem="xb%d" % s, ndma=1)
            P.op("pool", lambda e, s=s: e.tensor_scalar(out=xa[s][:], in0=xa[s][:], scalar1=selt[:, 0:1], scalar2=None, op0=ALU.mult),
                 reads=[("xa", s), "selt"], writes=[("xa", s)])
            P.op("dve", lambda e, s=s: e.scalar_tensor_tensor(out=xa[s][:], in0=xb[s][:], scalar=selt[:, 1:2], in1=xa[s][:],
                                                              op0=ALU.mult, op1=ALU.add),
                 reads=[("xa", s), ("xb", s), "selt"], writes=[("xa", s)])
            P.op("act", lambda e, j=j, s=s: e.dma_start(out=xs_d[j * 512:(j + 1) * 512, :].rearrange("(s p) c -> p s c", p=128),
                                                        in_=xa[s][:]), reads=[("xa", s)], sem="xa%d" % s, ndma=1)
        P.flush()


def build_fused(S):
    NT = S // 2
    nc = _nc()
    di = lambda name, shape, dt: nc.dram_tensor(name, shape, dt, kind="ExternalInput")
    dn = lambda name, shape, dt: nc.dram_tensor(name, shape, dt, kind="Internal")
    x = di("x", [S, D], F32)
    gattn = di("gattn", [2 * D], F32)
    wqk = di("wqk", [2, 2, D, 1024], F32)
    wv = di("wv", [2, 2, D, 512], F32)
    wg = di("wg", [2, D, 2048], F32)
    bg = di("bg", [2, 128, 16], F32)
    cos = di("cos", [128, S], F32)
    sin = di("sin", [128, S], F32)
    lam = di("lam", [2 * 256], F32)
    subln = di("subln", [2 * 128], F32)
    lcfg = di("lcfg", [2 * 2], F32)
    wo = di("wo", [2, 1024, D], F32)
    wout = di("wout", [2, D, D], F32)
    wfi = di("wfi", [2, D, 2 * FFN], F32)
    wfo = di("wfo", [2, FFN, D], F32)
    gffn = di("gffn", [2 * D], F32)
    gfin = di("gfin", [D], F32)
    consts = di("consts", [128, NCONST], BF16)
    sel = di("sel", [2], F32)
    mtI = di("mtI", [128, 8, 512], BF16)
    mtS = di("mtS", [128, 8, 512], BF16)
    o_all1 = dn("o_all1", [2, 512, NT], BF16)
    gT_sel = dn("gT_sel", [2048, NT], BF16)
    xsel = dn("xsel", [NT, D], F32)
    hT_all = dn("hT_all", [2, D, NT], BF16)
    QK = dn("QK", [1024, S], BF16)
    V = dn("V", [S, 512], BF16)
    o_all = dn("o_all", [2, 2, 512, NT], BF16)
    gT_all = dn("gT_all", [2, 2048, NT], BF16)
    x1 = dn("x1", [S, D], F32)
    WCO = dn("WCO", [1024, D], BF16)
    WCOUT = dn("WCOUT", [D, D], BF16)
    WCI = dn("WCI", [D, 2 * FFN], BF16)
    WCFO = dn("WCFO", [FFN, D], BF16)
    xo = nc.dram_tensor("xo", [NT, D], F32, kind="ExternalOutput")
    with ExitStack() as st:
        P = Prog(nc, st)
        for l in range(2):
            xin = x.ap() if l == 0 else x1.ap()
            ilB = None if l == 0 else dict(sel=sel, mtI=mtI.ap(), mtS=mtS.ap())
            for th in range(2):
                phase_A1(P, nc, NT, xin[th * NT:(th + 1) * NT, :], gattn, consts.ap(), hT_all.ap()[th], g_off=l * D)
            for hh in range(2):
                phase_A2(P, nc, S, hT_all.ap(), hT_all.ap()[hh], wqk.ap()[l, hh], wv.ap()[l, hh], wg.ap()[l], bg.ap()[l],
                         cos.ap(), sin.ap(), QK.ap(), V.ap(), gT_all.ap()[hh] if l == 0 else gT_sel.ap(),
                         il=None if l == 0 else dict(sel=sel, vt0=hh * (NT // 1024)))
                phase_B(P, nc, S, QK.ap(), V.ap(), lam, l * 256, subln, l * 128, (lcfg, l * 2), consts.ap(),
                        o_all.ap()[hh] if l == 0 else o_all1.ap()[hh], il=ilB)
            phase_C0(P, nc, [(wo.ap()[l], WCO.ap(), 1024, D), (wout.ap()[l], WCOUT.ap(), D, D),
                             (wfi.ap()[l], WCI.ap(), D, 2 * FFN), (wfo.ap()[l], WCFO.ap(), FFN, D)])
            if l == 0:
                for th in range(2):
                    phase_C1(P, nc, NT, o_all.ap()[:, th], gT_all.ap()[th], xin[th * NT:(th + 1) * NT, :], WCO.ap(), WCOUT.ap(),
                             WCI.ap(), WCFO.ap(), gffn, l * D, gfin, consts.ap(), x1.ap()[th * NT:(th + 1) * NT, :], final=False)
            else:
                phase_XSEL(P, nc, NT, xin, sel, xsel.ap())
                phase_C1(P, nc, NT, o_all1.ap(), gT_sel.ap(), xsel.ap(), WCO.ap(), WCOUT.ap(), WCI.ap(), WCFO.ap(), gffn, l * D,
                         gfin, consts.ap(), xo.ap(), final=True)
    return nc


def kernel_fused(inp, S):
    B = inp["x"].shape[0]
    assert 2 * B == N_CORES
    consts = host_consts()
    cosT, sinT = rope_tables(S)
    sl = [[slice_weights(inp, l, hh) for hh in range(2)] for l in range(2)]
    shared = dict(
        gattn=_f32(np.asarray(inp["norm_attn"]).reshape(-1)),
        wqk=_f32(np.stack([np.stack([sl[l][hh]["wqk"] for hh in range(2)]) for l in range(2)])),
        wv=_f32(np.stack([np.stack([sl[l][hh]["wv"] for hh in range(2)]) for l in range(2)])),
        wg=_f32(np.stack([sl[l][0]["wg"] for l in range(2)])),
        bg=_f32(np.stack([np.asarray(inp["b_gate"][l]).reshape(16, 128).T for l in range(2)])),
        cos=cosT, sin=sinT,
        lam=_f32(np.asarray(inp["diff_lambda"]).reshape(-1)),
        subln=_f32(np.asarray(inp["diff_subln"]).reshape(-1)),
        lcfg=np.array([v for l in range(2) for v in (-(0.8 - 0.6 * math.exp(-0.3 * l)), 1.0 - (0.8 - 0.6 * math.exp(-0.3 * l)))],
                      np.float32),
        wo=_f32(np.concatenate([inp["w_o_diff"], inp["w_o_sb"]], axis=1)),
        wout=_f32(inp["w_out"]), wfi=_f32(inp["w_ffn_in"]), wfo=_f32(inp["w_ffn_out"]),
        gffn=_f32(np.asarray(inp["norm_ffn"]).reshape(-1)), gfin=_f32(inp["norm_final"]), consts=consts)
    nc = build_fused(S)
    jj = np.arange(128)[:, None]
    tt = np.arange(512)[None, :]
    mt = {}
    for hf in range(2):
        mi = np.zeros((128, 8, 512), np.float32)
        ms = np.zeros((128, 8, 512), np.float32)
        for m_ in range(8):
            off = m_ * 128 - hf * 512
            mi[:, m_, :] = (tt - off - jj >= 0)
            ms[:, m_, :] = (tt - off - jj > 0)
        mt[hf] = (mi.astype(NPBF), ms.astype(NPBF))
    maps = []
    for c in range(N_CORES):
        hf = c % 2
        m = dict(shared)
        m["x"] = _f32(inp["x"][c // 2])
        m["sel"] = np.array([1.0 - hf, float(hf)], np.float32)
        m["mtI"], m["mtS"] = mt[hf]
        maps.append(m)
    res = _run(nc, maps)
    NT = S // 2
    out = np.zeros((B, S, D), np.float32)
    for c in range(N_CORES):
        b, hf = c // 2, c % 2
        xo = res[c]["xo"]
        for j in range(NT // 512):
            out[b, (2 * j + hf) * 512:(2 * j + hf + 1) * 512] = xo[j * 512:(j + 1) * 512]
    return out


def kernel(**inputs):
    inp = {k: np.asarray(v) for k, v in inputs.items()}
    return kernel_fused(inp, inp["x"].shape[1])
```

```python
import math
from contextlib import ExitStack

import ml_dtypes
import numpy as np

import concourse.bass as bass
import concourse.mybir as mybir
from concourse.bass_utils import run_bass_kernel_spmd

F32 = mybir.dt.float32
BF16 = mybir.dt.bfloat16
AF = mybir.ActivationFunctionType
ALU = mybir.AluOpType
NPBF = ml_dtypes.bfloat16

D = 1024
FFN = 2816
NF = FFN // 128
EPS = 1e-6
N_CORES = 8


class Op:
    __slots__ = ("eng", "fn", "waits", "signal", "value", "ndma", "sem", "is_dma")


class Prog:
    ENGS = ("sp", "act", "pool", "dve", "pe")

    def __init__(self, nc, stack, n_dma_sems=100):
        self.nc = nc
        self.q = {e: [] for e in self.ENGS}
        self.esem = {e: stack.enter_context(nc.semaphore("es_" + e)) for e in ("act", "pool", "dve", "pe")}
        self.ecount = {e: 0 for e in self.esem}
        self.stack = stack
        self.dma_sem_of = {}
        self.dma_count = {}
        self.last_writer = {}
        self.readers = {}
        self.waited = {e: {} for e in self.ENGS}

    def _dsem(self, key):
        if key not in self.dma_sem_of:
            idx = len(self.dma_sem_of)
            self.dma_sem_of[key] = self.stack.enter_context(self.nc.semaphore("ds%d" % idx))
            self.dma_count[key] = 0
        return self.dma_sem_of[key]

    def op(self, eng, fn, reads=(), writes=(), sem=None, ndma=0):
        o = Op()
        o.eng = eng
        o.fn = fn
        o.signal = False
        o.value = None
        o.ndma = ndma
        o.is_dma = ndma > 0
        o.sem = None
        deps = []
        for k in reads:
            lw = self.last_writer.get(k)
            if lw is not None:
                deps.append(lw)
        for k in writes:
            lw = self.last_writer.get(k)
            if lw is not None:
                deps.append(lw)
            rs = self.readers.get(k)
            if rs:
                deps.extend(rs)
        waits = []
        seen = set()
        for d in deps:
            if id(d) in seen:
                continue
            seen.add(id(d))
            if (not d.is_dma) and d.eng == "pe" and eng == "pe" and not o.is_dma:
                continue
            if not d.is_dma:
                d.signal = True
            waits.append(d)
        o.waits = waits
        if o.is_dma:
            o.sem = self._dsem(sem)
            self.dma_count[sem] += 16 * ndma
            o.value = self.dma_count[sem]
        for k in writes:
            self.last_writer[k] = o
            self.readers[k] = []
        for k in reads:
            self.readers.setdefault(k, []).append(o)
        self.q[eng].append(o)
        return o

    def flush(self):
        nc = self.nc
        for e in self.esem:
            c = self.ecount[e]
            for o in self.q[e]:
                if o.is_dma:
                    continue
                if o.signal:
                    c += 1
                    o.value = c
            self.ecount[e] = c
        dma_final = [(self.dma_sem_of[k], self.dma_count[k]) for k in self.dma_sem_of]

        def emit(e, eng):
            wd = self.waited[eng]
            for o in self.q[eng]:
                for d in o.waits:
                    s = d.sem if d.is_dma else self.esem[d.eng]
                    v = d.value
                    if wd.get(id(s), 0) < v:
                        e.wait_ge(s, v)
                        wd[id(s)] = v
                if o.fn is None:
                    continue
                r = o.fn(e)
                if o.is_dma:
                    if not isinstance(r, (list, tuple)):
                        r = [r]
                    assert len(r) == o.ndma, (len(r), o.ndma)
                    for ins in r:
                        ins.then_inc(o.sem, 16)
                elif o.signal:
                    r.then_inc(self.esem[eng], 1)
            if eng == "sp":
                for s, v in dma_final:
                    if v > 0 and wd.get(id(s), 0) < v:
                        e.wait_ge(s, v)
                        wd[id(s)] = v

        with nc.Block() as block:
            decos = {"sp": block.sync, "act": block.scalar, "pool": block.gpsimd, "dve": block.vector,
                     "pe": block.tensor}
            for eng in self.ENGS:
                if self.q[eng] or eng == "sp":
                    decos[eng](lambda e, eng=eng: emit(e, eng))
        self.q = {e: [] for e in self.ENGS}
        self.last_writer = {}
        self.readers = {}


class Rot:
    def __init__(self, n):
        self.n = n
        self.i = -1

    def next(self):
        self.i = (self.i + 1) % self.n
        return self.i


def bcast_ap(handle, n, offset=0):
    return bass.AP(handle, offset, [[0, 128], [1, n]])


C_ID, C_TRI, C_ONE, C_MS, C_MI, NCONST = 0, 128, 256, 384, 1280, 2176


def host_consts():
    c = np.zeros((128, NCONST), np.float32)
    j = np.arange(128)[:, None]
    c[:, C_ID:C_ID + 128] = (j == np.arange(128)[None, :])
    c[:, C_TRI:C_TRI + 128] = (j >= np.arange(128)[None, :])
    c[:, C_ONE:C_ONE + 128] = 1.0
    u = np.arange(896)[None, :] - 384
    c[:, C_MS:C_MS + 896] = (u - j > 0)
    c[:, C_MI:C_MI + 896] = (u - j >= 0)
    return c.astype(NPBF)


def rope_tables(S):
    pos = np.arange(S, dtype=np.float32)
    inv = (np.float32(10000.0) ** (-np.arange(0, 64, 2, dtype=np.float32) / np.float32(64))).astype(np.float32)
    ang = (pos[:, None] * inv[None, :]).astype(np.float32)
    cos = np.cos(ang).astype(np.float32).T
    sin = np.sin(ang).astype(np.float32).T
    cosT = np.ascontiguousarray(np.tile(cos, (4, 1)))
    sinT = np.ascontiguousarray(np.tile(sin, (4, 1)))
    return cosT, sinT


class Alloc:
    _n = [0]

    def __init__(self, nc, st, prefix):
        Alloc._n[0] += 1
        self.nc, self.st, self.prefix = nc, st, "p%d_%s" % (Alloc._n[0], prefix)

    def sb(self, name, shape, dt):
        return self.st.enter_context(self.nc.sbuf_tensor(self.prefix + name, shape, dt))

    def ps(self, name, shape, dt):
        return self.st.enter_context(self.nc.psum_tensor(self.prefix + name, shape, dt))


def phase_A1(P, nc, NT, x_d, g_h, consts_d, hT_d, g_off=0):
    with ExitStack() as st:
        A = Alloc(nc, st, "a1_")
        xt = [A.sb("x%d" % i, [128, 1024], F32) for i in range(3)]
        gb = A.sb("gb", [128, 1024], F32)
        junk = A.sb("junk", [128, 1024], BF16)
        ssq = A.sb("ssq", [128, 1], F32)
        rstd = A.sb("rstd", [128, 1], F32)
        hb = A.sb("hb", [128, 1024], BF16)
        cst = A.sb("cst", [128, NCONST], BF16)
        hTs = [A.sb("hT%d" % i, [128, 8, 512], BF16) for i in range(2)]
        psT = A.ps("psT", [128, 1024], BF16)
        P.op("sp", lambda e: e.dma_start(out=cst[:], in_=consts_d), writes=["consts"], sem="cst", ndma=1)
        P.op("sp", lambda e: e.dma_start(out=gb[:], in_=bcast_ap(g_h, 1024, g_off)), writes=["gb"], sem="gb", ndma=1)
        hT_v = hT_d.rearrange("(c p) t -> p c t", p=128)
        for i in range(NT // 128):
            s = i % 3
            P.op("sp", lambda e, i=i, s=s: e.dma_start(out=xt[s][:], in_=x_d[i * 128:(i + 1) * 128, :]),
                 writes=["x%d" % s], sem="x%d" % s, ndma=1)
            hs, sub = (i // 4) % 2, i % 4
            emit_norm_transpose2(P, ["x%d" % s], xt[s][:], gb, junk, ssq, rstd, hb, psT, cst[:, C_ID:C_ID + 128],
                                hTs[hs][:, :, sub * 128:(sub + 1) * 128], [("hTs", hs, sub)])
            if sub == 3:
                t0 = (i // 4) * 512
                P.op("act", lambda e, hs=hs, t0=t0: e.dma_start(out=hT_v[:, :, t0:t0 + 512], in_=hTs[hs][:]),
                     reads=[("hTs", hs, k) for k in range(4)], sem="hTs%d" % hs, ndma=1)
        P.flush()


WA_DQ, WA_DQR, WA_DK, WA_DKR, WA_SQ, WA_SK, WA_V, WA_G, WA_N = 0, 256, 512, 768, 1024, 1280, 1536, 2048, 4096


def phase_A2(P, nc, S, hT_all_d, hT_mine_d, wqk_d, wv_d, wg_d, bg_d, cos_d, sin_d, QK_d, V_d, gT_d, il=None):
    NT = S // 2
    with ExitStack() as st:
        A = Alloc(nc, st, "a2_")
        WA = A.sb("WA", [128, 8, WA_N], BF16)
        stg = [A.sb("stg%d" % i, [128, 8, 512], F32) for i in range(2)]
        ht = [A.sb("ht%d" % i, [128, 8, 512], BF16) for i in range(2)]
        cs = [A.sb("cs%d" % i, [128, 2, 512], F32) for i in range(2)]
        t1 = [A.sb("t1_%d" % i, [128, 512], F32) for i in range(2)]
        t2 = [A.sb("t2_%d" % i, [128, 512], F32) for i in range(2)]
        QKs = [A.sb("QKs%d" % i, [128, 8, 512], BF16) for i in range(2)]
        Vs = [A.sb("Vs%d" % i, [128, 4, 512], BF16) for i in range(2)]
        Gs = [A.sb("Gs%d" % i, [128, 8, 512], BF16) for i in range(2)]
        bg = A.sb("bg", [128, 16], F32)
        ps = A.ps("ps", [128, 6, 512], F32)
        bank = Rot(6)
        P.op("sp", lambda e: e.dma_start(out=bg[:], in_=bg_d), writes=["bg"], sem="bg", ndma=1)

        def wview(w_d):
            return w_d.rearrange("(k p) c -> p k c", p=128)
        groups = [(wview(wqk_d), 0), (wview(wqk_d), 512), (wview(wv_d), 0)] + [(wview(wg_d), i * 512) for i in range(4)]
        ceng = Rot(2)
        for gi, (wv_, c0) in enumerate(groups):
            s = gi % 2
            P.op("sp", lambda e, wv_=wv_, c0=c0, s=s: e.dma_start(out=stg[s][:], in_=wv_[:, :, c0:c0 + 512]),
                 writes=[("stg", s)], sem="stg%d" % s, ndma=1)

            def cast(dst0, src0, n, scale, s=s):
                eng = ("dve", "pool")[ceng.next()]
                P.op(eng, lambda e: e.tensor_scalar(out=WA[:, :, dst0:dst0 + n], in0=stg[s][:, :, src0:src0 + n],
                                                    scalar1=float(scale), scalar2=None, op0=ALU.mult),
                     reads=[("stg", s)], writes=[("WA", dst0)])
            if gi == 0:
                cast(WA_DQ, 0, 256, 0.125)
                cast(WA_DK, 256, 256, 1.0)
                for sec, src, sc in ((WA_DQR, 0, 0.125), (WA_DKR, 256, 1.0)):
                    for b in range(4):
                        cast(sec + b * 64, src + b * 64 + 32, 32, -sc)
                        cast(sec + b * 64 + 32, src + b * 64, 32, sc)
            elif gi == 1:
                cast(WA_SQ, 0, 256, 0.125)
                cast(WA_SK, 256, 256, 1.0)
            elif gi == 2:
                cast(WA_V, 0, 512, 1.0)
            else:
                cast(WA_G + (gi - 3) * 512, 0, 512, 1.0)
        WAK = [k for k in P.last_writer if isinstance(k, tuple) and k[0] == "WA"]

        QK_v = QK_d.rearrange("(c p) t -> p c t", p=128)
        V_v = V_d.rearrange("(s p) c -> p s c", p=128)
        ntt = NT // 512
        def ld_main(T):
            s = T % 2
            r, tt = T // ntt, T % ntt
            hv = hT_all_d[r].rearrange("(c p) t -> p c t", p=128)
            P.op("sp", lambda e, hv=hv, tt=tt, s=s: e.dma_start(out=ht[s][:], in_=hv[:, :, tt * 512:(tt + 1) * 512]),
                 writes=[("ht", s)], sem="ht%d" % s, ndma=1)
            P.op("sp", lambda e, T=T, s=s: [e.dma_start(out=cs[s][:, 0, :], in_=cos_d[:, T * 512:(T + 1) * 512]),
                                            e.dma_start(out=cs[s][:, 1, :], in_=sin_d[:, T * 512:(T + 1) * 512])],
                 writes=[("cs", s)], sem="cs%d" % s, ndma=2)

        ld_main(0)
        for T in range(S // 512):
            s = T % 2
            if T + 1 < S // 512:
                ld_main(T + 1)

            def mm8(b, col0, s=s):
                for k in range(8):
                    P.op("pe", lambda e, k=k: e.matmul(ps[:, b, :], lhsT=WA[:, k, col0:col0 + 128], rhs=ht[s][:, k, :],
                                                       start=(k == 0), stop=(k == 7)),
                         reads=[("ht", s)] + (WAK if k == 0 else []), writes=[("ps", b)])
            for ch in range(4):
                col0 = (WA_DQ if ch < 2 else WA_DK) + (ch % 2) * 128
                ba, bb = bank.next(), bank.next()
                mm8(ba, col0)
                mm8(bb, col0 + 256)
                ts_ = ch % 2
                P.op("dve", lambda e, ba=ba, ts_=ts_, s=s: e.tensor_tensor(out=t1[ts_][:], in0=ps[:, ba, :], in1=cs[s][:, 0, :],
                                                                         op=ALU.mult),
                     reads=[("ps", ba), ("cs", s)], writes=[("t1", ts_)])
                P.op("dve", lambda e, bb=bb, ts_=ts_, s=s: e.tensor_tensor(out=t2[ts_][:], in0=ps[:, bb, :], in1=cs[s][:, 1, :],
                                                                         op=ALU.mult),
                     reads=[("ps", bb), ("cs", s)], writes=[("t2", ts_)])
                P.op("pool", lambda e, ch=ch, ts_=ts_, s=s: e.tensor_tensor(out=QKs[s][:, ch, :], in0=t1[ts_][:], in1=t2[ts_][:],
                                                                          op=ALU.add),
                     reads=[("t1", ts_), ("t2", ts_)], writes=[("QKs", s, ch)])
            for ch in range(4, 8):
                col0 = WA_SQ + (ch - 4) * 128
                b = bank.next()
                mm8(b, col0)
                P.op("act", lambda e, b=b, ch=ch, s=s: e.copy(out=QKs[s][:, ch, :], in_=ps[:, b, :]),
                     reads=[("ps", b)], writes=[("QKs", s, ch)])
            P.op("sp", lambda e, T=T, s=s: e.dma_start(out=QK_v[:, :, T * 512:(T + 1) * 512], in_=QKs[s][:]),
                 reads=[("QKs", s, ch) for ch in range(8)], sem="QKs%d" % s, ndma=1)
            for sub in range(4):
                b = bank.next()
                for k in range(8):
                    P.op("pe", lambda e, k=k, b=b, sub=sub, s=s: e.matmul(ps[:, b, :], lhsT=ht[s][:, k, sub * 128:(sub + 1) * 128],
                                                                        rhs=WA[:, k, WA_V:WA_V + 512],
                                                                        start=(k == 0), stop=(k == 7)),
                         reads=[("ht", s)], writes=[("ps", b)])
                P.op("act", lambda e, b=b, sub=sub, s=s: e.copy(out=Vs[s][:, sub, :], in_=ps[:, b, :]),
                     reads=[("ps", b)], writes=[("Vs", s, sub)])
            P.op("sp", lambda e, T=T, s=s: e.dma_start(out=V_v[:, T * 4:(T + 1) * 4, :], in_=Vs[s][:]),
                 reads=[("Vs", s, k) for k in range(4)], sem="Vs%d" % s, ndma=1)

        hm = hT_mine_d.rearrange("(c p) t -> p c t", p=128)
        gT_v = gT_d.rearrange("(c p) t -> p c t", p=128)
        gsl = Rot(2)
        if il is not None:
            selt = A.sb("selt", [128, 2], F32)
            htb = A.sb("htb", [128, 8, 512], BF16)
            P.op("sp", lambda e: e.dma_start(out=selt[:], in_=bcast_ap(il["sel"], 2)), writes=["selt"], sem="selt", ndma=1)
        gtiles = list(range(ntt) if il is None else range(il["vt0"], il["vt0"] + ntt // 2))

        def ld_gate(T):
            s = T % 2
            if il is None:
                P.op("sp", lambda e, T=T, s=s: e.dma_start(out=ht[s][:], in_=hm[:, :, T * 512:(T + 1) * 512]),
                     writes=[("ht", s)], sem="ht%d" % s, ndma=1)
            else:
                ra, rb_ = 2 * T, 2 * T + 1
                hva = hT_all_d[ra // ntt].rearrange("(c p) t -> p c t", p=128)
                hvb = hT_all_d[rb_ // ntt].rearrange("(c p) t -> p c t", p=128)
                P.op("sp", lambda e, hva=hva, ra=ra, s=s: e.dma_start(out=ht[s][:], in_=hva[:, :, (ra % ntt) * 512:(ra % ntt + 1) * 512]),
                     writes=[("ht", s)], sem="ht%d" % s, ndma=1)
                P.op("sp", lambda e, hvb=hvb, rb_=rb_: e.dma_start(out=htb[:], in_=hvb[:, :, (rb_ % ntt) * 512:(rb_ % ntt + 1) * 512]),
                     writes=["htb"], sem="htb", ndma=1)
                P.op("dve", lambda e, s=s: e.tensor_scalar(out=ht[s][:], in0=ht[s][:], scalar1=selt[:, 0:1], scalar2=None, op0=ALU.mult),
                     reads=[("ht", s), "selt"], writes=[("ht", s)])
                P.op("dve", lambda e, s=s: e.scalar_tensor_tensor(out=ht[s][:], in0=htb[:], scalar=selt[:, 1:2], in1=ht[s][:],
                                                                  op0=ALU.mult, op1=ALU.add),
                     reads=[("ht", s), "htb", "selt"], writes=[("ht", s)])

        ld_gate(gtiles[0])
        for gi_, T in enumerate(gtiles):
            s = T % 2
            if gi_ + 1 < len(gtiles):
                ld_gate(gtiles[gi_ + 1])
            for half in range(2):
                g = gsl.next()
                for c8 in range(8):
                    ch = half * 8 + c8
                    b = bank.next()
                    for k in range(8):
                        P.op("pe", lambda e, k=k, b=b, ch=ch, s=s: e.matmul(ps[:, b, :],
                                                                          lhsT=WA[:, k, WA_G + ch * 128:WA_G + (ch + 1) * 128],
                                                                          rhs=ht[s][:, k, :], start=(k == 0), stop=(k == 7)),
                             reads=[("ht", s)], writes=[("ps", b)])
                    P.op("act", lambda e, b=b, ch=ch, c8=c8, g=g: e.activation(out=Gs[g][:, c8, :], in_=ps[:, b, :],
                                                                              func=AF.Sigmoid, bias=bg[:, ch:ch + 1]),
                         reads=[("ps", b), "bg"], writes=[("Gs", g, c8)])
                P.op("sp", lambda e, T=T, half=half, g=g: e.dma_start(out=gT_v[:, half * 8:(half + 1) * 8, T * 512:(T + 1) * 512],
                                                                    in_=Gs[g][:]),
                     reads=[("Gs", g, k) for k in range(8)], sem="Gs%d" % g, ndma=1)
        P.flush()


def phase_B(P, nc, S, QK_d, V_d, lam_h, lam_off, subln_h, subln_off, lam_init, consts_d, o_send_d, il=None):
    NT = S // 2
    NCH = S // 128
    NQT = S // 512
    ntt = NT // 512
    with ExitStack() as st:
        A = Alloc(nc, st, "b_")
        cst = A.sb("cst", [128, NCONST], BF16)
        qk = [A.sb("qk%d" % i, [64, S], BF16) for i in range(4)]
        Vd = A.sb("Vd", [128, NCH, 129], BF16)
        Vsb = A.sb("Vsb", [128, NCH, 64], BF16)
        Pm = [A.sb("Pm%d" % i, [128, 1024], BF16) for i in range(3)]
        Et = [A.sb("E%d" % i, [128, 1024], BF16) for i in range(4)]
        spt = [A.sb("sp%d" % i, [128, 1024], BF16) for i in range(3)]
        cst_ = [A.sb("cs%d" % i, [128, 512], BF16) for i in range(4)]
        xct = [A.sb("xc%d" % i, [128, 1024], BF16) for i in range(2)]
        At = [A.sb("A%d" % i, [128, 1024], BF16) for i in range(3)]
        o1 = A.sb("o1", [128, 128], F32)
        oo = A.sb("oo", [128, 128], F32)
        on = A.sb("on", [128, 128], BF16)
        junk = A.sb("junk", [128, 128], BF16)
        rr = A.sb("rr", [128, 4], F32)
        ssq = A.sb("ssq", [128, 1], F32)
        rstd = A.sb("rstd", [128, 1], F32)
        lp = A.sb("lp", [128, 256], F32)
        lt = A.sb("lt", [128, 128], F32)
        lsum = A.sb("lsum", [128, 2], F32)
        nlam = A.sb("nlam", [128, 1], F32)
        sublnb = A.sb("sublnb", [128, 128], F32)
        odT = [A.sb("odT%d" % i, [128, 512], BF16) for i in range(2)]
        osT = [A.sb("osT%d" % i, [64, 512], BF16) for i in range(2)]
        ps = A.ps("ps", [128, 8, 512], F32)
        rl = A.sb("rl", [128, 2, 512], F32)
        Psm = [A.sb("Psm%d" % i, [128, 512], BF16) for i in range(3)]
        psr = Rot(3)
        o1w = A.sb("o1w", [128, 512], F32)
        o2w = A.sb("o2w", [128, 512], F32)
        oow = A.sb("oow", [128, 512], F32)
        sqw = A.sb("sqw", [128, 512], BF16)
        rrw = A.sb("rrw", [128, 512], F32)
        sublnc = A.sb("sublnc", [128, 1], F32)
        ident = cst[:, C_ID:C_ID + 128]
        tri = cst[:, C_TRI:C_TRI + 128]
        ones = cst[:, C_ONE:C_ONE + 128]

        P.op("sp", lambda e: e.dma_start(out=cst[:], in_=consts_d), writes=["consts"], sem="cst", ndma=1)
        if il is None:
            NVT = NQT
            nch = lambda qt: 4 * qt + 4
            mbase = lambda qt: 4 * qt
            mask_ap = lambda kind, m: cst[:, (C_MI if kind == "I" else C_MS) + 384 - m * 128:
                                          (C_MI if kind == "I" else C_MS) + 384 - m * 128 + 512]
            qsrc = lambda i: qk[i]
            o_dst = lambda qt, r0, r1: o_send_d[qt // ntt, r0:r1, (qt % ntt) * 512:(qt % ntt + 1) * 512]
        else:
            NVT = NQT // 2
            nch = lambda qt: 8 * qt + 8
            mbase = lambda qt: 8 * qt
            MtI = A.sb("MtI", [128, 8, 512], BF16)
            MtS = A.sb("MtS", [128, 8, 512], BF16)
            selt = A.sb("selt", [128, 2], F32)
            qsel = [A.sb("qsel%d" % i, [64, NT], BF16) for i in range(2)]
            qtmp = A.sb("qtmp", [64, NT], BF16)
            P.op("sp", lambda e: e.dma_start(out=MtI[:], in_=il["mtI"]), writes=["consts2"], sem="MtI", ndma=1)
            P.op("sp", lambda e: e.dma_start(out=MtS[:], in_=il["mtS"]), writes=["consts3"], sem="MtS", ndma=1)
            P.op("sp", lambda e: e.dma_start(out=selt[:], in_=bcast_ap(il["sel"], 2)), writes=["selt"], sem="selt", ndma=1)
            mask_ap = lambda kind, m: (MtI if kind == "I" else MtS)[:, m, :]
            qsrc = lambda i: qsel[i // 2]
            o_dst = lambda qt, r0, r1: o_send_d[r0:r1, qt * 512:(qt + 1) * 512]

            def blend_q(i):
                v = qk[i][:].rearrange("p (j two t) -> p j two t", two=2, t=512)
                P.op("dve", lambda e: e.tensor_scalar(out=qtmp[:].rearrange("p (j t) -> p j t", t=512), in0=v[:, :, 0, :],
                                                      scalar1=selt[0:64, 0:1], scalar2=None, op0=ALU.mult),
                     reads=[("qk", i), "selt"], writes=["qtmp"])
                P.op("dve", lambda e: e.scalar_tensor_tensor(out=qsel[i // 2][:].rearrange("p (j t) -> p j t", t=512),
                                                             in0=v[:, :, 1, :], scalar=selt[0:64, 1:2],
                                                             in1=qtmp[:].rearrange("p (j t) -> p j t", t=512),
                                                             op0=ALU.mult, op1=ALU.add),
                     reads=[("qk", i), "selt", "qtmp"], writes=[("qsel", i // 2)])
        P.op("sp", lambda e: e.dma_start(out=lp[:], in_=bcast_ap(lam_h, 256, lam_off)), writes=["lp"], sem="lp", ndma=1)
        P.op("sp", lambda e: e.dma_start(out=sublnb[:], in_=bcast_ap(subln_h, 128, subln_off)), writes=["sublnb"],
             sem="sublnb", ndma=1)
        P.op("dve", lambda e: e.tensor_tensor(out=lt[:].rearrange("p (a b) -> p a b", a=2),
                                              in0=lp[:].rearrange("p (a b c) -> p a b c", a=2, b=2)[:, :, 0, :],
                                              in1=lp[:].rearrange("p (a b c) -> p a b c", a=2, b=2)[:, :, 1, :],
                                              op=ALU.mult), reads=["lp"], writes=["lt"])
        P.op("dve", lambda e: e.reduce_sum(out=lsum[:], in_=lt[:].rearrange("p (a b) -> p a b", a=2),
                                           axis=mybir.AxisListType.X), reads=["lt"], writes=["lsum"])
        P.op("act", lambda e: e.activation(out=lsum[:], in_=lsum[:], func=AF.Exp), reads=["lsum"], writes=["lsum"])
        P.op("dve", lambda e: e.tensor_tensor(out=nlam[:], in0=lsum[:, 1:2], in1=lsum[:, 0:1], op=ALU.subtract),
             reads=["lsum"], writes=["nlam"])
        lcf = A.sb("lcf", [128, 2], F32)
        lc_h, lc_off = lam_init if isinstance(lam_init, tuple) else (lam_init, 0)
        P.op("sp", lambda e: e.dma_start(out=lcf[:], in_=bcast_ap(lc_h, 2, lc_off)), writes=["lcf"], sem="lcf", ndma=1)
        P.op("dve", lambda e: e.tensor_tensor(out=nlam[:], in0=nlam[:], in1=lcf[:, 0:1], op=ALU.add),
             reads=["nlam", "lcf"], writes=["nlam"])
        P.op("dve", lambda e: e.tensor_scalar(out=sublnb[:], in0=sublnb[:], scalar1=lcf[:, 1:2], scalar2=None,
                                              op0=ALU.mult), reads=["sublnb", "lcf"], writes=["sublnb"])
        P.op("sp", lambda e: e.dma_start(out=sublnc[:], in_=bass.AP(subln_h, subln_off, [[1, 128], [1, 1]])),
             writes=["sublnc"], sem="sublnc", ndma=1)
        P.op("dve", lambda e: e.tensor_tensor(out=sublnc[:], in0=sublnc[:], in1=lcf[:, 1:2], op=ALU.mult),
             reads=["sublnc", "lcf"], writes=["sublnc"])

        V_v = V_d.rearrange("(c p) d -> p c d", p=128)

        sp_slot = Rot(2)
        pmr = Rot(3)
        odr = Rot(2)
        ptr = Rot(8)

        def acc_ap(comp, qs):
            r = comp * 4 + qs
            return ps[:, 4 + r // 3, (r % 3) * 136:(r % 3) * 136 + 129]

        def acc_first(comp, qs):
            return (comp * 4 + qs) % 3 == 0

        for hd in range(2):
            for i in range(4):
                row0 = (0 if i % 2 == 0 else 256) + (hd * 2 + i // 2) * 64
                P.op("sp", lambda e, i=i, row0=row0: e.dma_start(out=qk[i][:], in_=QK_d[row0:row0 + 64, :]),
                     writes=[("qk", i)], sem="qk%d" % i, ndma=1)
            P.op("sp", lambda e, hd=hd: e.dma_start(out=Vd[:, :, 0:128], in_=V_v[:, :, hd * 128:(hd + 1) * 128]),
                 writes=["Vd"], sem="Vd", ndma=1)
            if il is not None:
                blend_q(0)
                blend_q(2)
            for qt in range(NVT):
                steps = [(comp, c0) for comp in range(2) for c0 in range(0, nch(qt), 2)]
                pend = None
                pendL = None

                def emitL(pcomp, pc0, pss, lastc):
                    P.op("pe", lambda e: e.matmul(ps[:, 6 + pcomp, :], lhsT=ones, rhs=Psm[pss][:], start=(pc0 == 0),
                                                  stop=(pc0 + 1 == lastc)),
                         reads=[("Psm", pss), "consts"], writes=[("LT", pcomp)])

                for item in steps + [None]:
                    cur = None
                    if item is not None:
                        comp, c0 = item
                        sb_ = sp_slot.next() * 2
                        pslot = pmr.next()
                        QT, KT = qsrc(2 * comp), qk[2 * comp + 1]
                        qkey = ("qk", 2 * comp) if il is None else ("qsel", comp)
                        for h in range(2):
                            c = c0 + h
                            P.op("pe", lambda e, b=sb_ + h, c=c, qt=qt, QT=QT, KT=KT: e.matmul(
                                ps[:, b, :], lhsT=KT[:, c * 128:(c + 1) * 128], rhs=QT[:, qt * 512:(qt + 1) * 512],
                                start=True, stop=True),
                                reads=[qkey, ("qk", 2 * comp + 1)], writes=[("ps", sb_ + h)])
                        P.op("act", lambda e, sb_=sb_, pslot=pslot: e.activation(
                            out=Pm[pslot][:].rearrange("p (a b) -> p a b", a=2), in_=ps[:, sb_:sb_ + 2, :], func=AF.Exp),
                            reads=[("ps", sb_), ("ps", sb_ + 1)], writes=[("Pm", pslot)])
                        for h in range(2):
                            j = c0 + h - mbase(qt)
                            if j >= 0:
                                P.op("dve", lambda e, pslot=pslot, j=j, h=h: e.tensor_tensor(
                                    out=Pm[pslot][:, h * 512:(h + 1) * 512], in0=Pm[pslot][:, h * 512:(h + 1) * 512],
                                    in1=mask_ap("I", j), op=ALU.mult),
                                    reads=[("Pm", pslot), "consts", "consts2"], writes=[("Pm", pslot)])
                        cur = (comp, c0, pslot)
                    if pend is not None:
                        pcomp, pc0, ppslot = pend
                        lastc = nch(qt) - 1
                        for h in range(2):
                            pc = pc0 + h
                            P.op("pe", lambda e, pcomp=pcomp, pc=pc, ppslot=ppslot, h=h, lastc=lastc: e.matmul(
                                ps[:, 4 + pcomp, :], lhsT=Vd[:, pc, 0:128], rhs=Pm[ppslot][:, h * 512:(h + 1) * 512],
                                start=(pc == 0), stop=(pc == lastc)),
                                reads=[("Pm", ppslot), "Vd"], writes=[("OT", pcomp)])
                        pss = psr.next()
                        P.op("dve", lambda e, ppslot=ppslot, pss=pss: e.tensor_tensor(out=Psm[pss][:], in0=Pm[ppslot][:, 0:512],
                                                                                   in1=Pm[ppslot][:, 512:1024], op=ALU.add),
                             reads=[("Pm", ppslot)], writes=[("Psm", pss)])
                        if pendL is not None:
                            emitL(*pendL)
                        pendL = (pcomp, pc0, pss, lastc)
                    pend = cur
                if pendL is not None:
                    emitL(*pendL)
                    pendL = None
                od = odr.next()
                for comp in range(2):
                    P.op("act", lambda e, comp=comp: e.activation(out=rl[:, comp, :], in_=ps[:, 6 + comp, :], func=AF.Ln),
                         reads=[("LT", comp)], writes=[("rl", comp)])
                    P.op("act", lambda e, comp=comp: e.activation(out=rl[:, comp, :], in_=rl[:, comp, :], func=AF.Exp, scale=-1.0),
                         reads=[("rl", comp)], writes=[("rl", comp)])
                P.op("dve", lambda e: e.tensor_tensor(out=o1w[:], in0=ps[:, 4, :], in1=rl[:, 0, :], op=ALU.mult),
                     reads=[("OT", 0), ("rl", 0)], writes=["o1w"])
                P.op("dve", lambda e: e.tensor_tensor(out=o2w[:], in0=ps[:, 5, :], in1=rl[:, 1, :], op=ALU.mult),
                     reads=[("OT", 1), ("rl", 1)], writes=["o2w"])
                P.op("dve", lambda e: e.scalar_tensor_tensor(out=oow[:], in0=o2w[:], scalar=nlam[:, 0:1], in1=o1w[:],
                                                             op0=ALU.mult, op1=ALU.add),
                     reads=["o1w", "o2w", "nlam"], writes=["oow"])
                P.op("act", lambda e: e.activation(out=sqw[:], in_=oow[:], func=AF.Square), reads=["oow"], writes=["sqw"])
                P.op("pe", lambda e: e.matmul(ps[:, 6, :], lhsT=ones, rhs=sqw[:], start=True, stop=True),
                     reads=["sqw", "consts"], writes=[("LT", 0)])
                P.op("act", lambda e: e.activation(out=rrw[:], in_=ps[:, 6, :], func=AF.Ln, scale=1.0 / 128, bias=EPS),
                     reads=[("LT", 0)], writes=["rrw"])
                P.op("act", lambda e: e.activation(out=rrw[:], in_=rrw[:], func=AF.Exp, scale=-0.5),
                     reads=["rrw"], writes=["rrw"])
                P.op("dve", lambda e, od=od: e.scalar_tensor_tensor(out=odT[od][:], in0=oow[:], scalar=sublnc[:, 0:1], in1=rrw[:],
                                                                    op0=ALU.mult, op1=ALU.mult),
                     reads=["oow", "rrw", "sublnc"], writes=[("odT", od)])
                P.op("sp", lambda e, od=od, hd=hd, qt=qt: e.dma_start(
                    out=o_dst(qt, hd * 128, (hd + 1) * 128), in_=odT[od][:]),
                    reads=[("odT", od)], sem="odT%d" % od, ndma=1)

        zslot = Rot(2)
        er, spr, csr, xr, ar, osr = Rot(4), Rot(3), Rot(4), Rot(2), Rot(3), Rot(2)
        QT, KT = qk[0], qk[1]
        for hs in range(4):
            P.op("sp", lambda e, hs=hs: e.dma_start(out=QT[:], in_=QK_d[512 + hs * 64:512 + (hs + 1) * 64, :]),
                 writes=[("qk", 0)], sem="qk0", ndma=1)
            P.op("sp", lambda e, hs=hs: e.dma_start(out=KT[:], in_=QK_d[768 + hs * 64:768 + (hs + 1) * 64, :]),
                 writes=[("qk", 1)], sem="qk1", ndma=1)
            P.op("sp", lambda e, hs=hs: e.dma_start(out=Vsb[:], in_=V_v[:, :, 256 + hs * 64:256 + (hs + 1) * 64]),
                 writes=["Vsb"], sem="Vsb", ndma=1)
            if il is not None:
                blend_q(0)
            QS = qsrc(0)
            qkey = ("qk", 0) if il is None else ("qsel", 0)
            steps = []
            for qt in range(NVT):
                for c1 in range(nch(qt) - 1, 0, -2):
                    steps.append((qt, c1))
            n = len(steps)
            info = [None] * n
            cs_cur = None
            for t in range(n + 3):
                if t < n:
                    qt, c1 = steps[t]
                    zb, es = zslot.next() * 2, er.next()
                    for h in range(2):
                        c = c1 - h
                        P.op("pe", lambda e, b=zb + h, c=c, qt=qt: e.matmul(
                            ps[:, b, :], lhsT=KT[:, c * 128:(c + 1) * 128],
                            rhs=QS[:, qt * 512:(qt + 1) * 512], start=True, stop=True),
                             reads=[qkey, ("qk", 1)], writes=[("ps", zb + h)])
                    P.op("act", lambda e, zb=zb, es=es: e.activation(out=Et[es][:].rearrange("p (a b) -> p a b", a=2),
                                                                     in_=ps[:, zb:zb + 2, :], func=AF.Exp),
                         reads=[("ps", zb), ("ps", zb + 1)], writes=[("E", es)])
                    for h in range(2):
                        j = c1 - h - mbase(qt)
                        if j >= 0:
                            P.op("dve", lambda e, es=es, j=j, h=h: e.tensor_tensor(
                                out=Et[es][:, h * 512:(h + 1) * 512], in0=Et[es][:, h * 512:(h + 1) * 512],
                                in1=mask_ap("S", j), op=ALU.mult),
                                reads=[("E", es), "consts", "consts3"], writes=[("E", es)])
                    info[t] = dict(qt=qt, c1=c1, es=es)
                if 1 <= t <= n:
                    d = info[t - 1]
                    ss = spr.next()
                    P.op("act", lambda e, es=d["es"], ss=ss: e.activation(out=spt[ss][:], in_=Et[es][:], func=AF.Ln, bias=1.0),
                         reads=[("E", d["es"])], writes=[("sp", ss)])
                    d["ss"] = ss
                if 2 <= t <= n + 1:
                    d = info[t - 2]
                    qt, c1, ss = d["qt"], d["c1"], d["ss"]
                    first = (c1 == nch(qt) - 1)
                    xs = xr.next()
                    rd = [("sp", ss), "consts"] + ([] if first else [("csum", cs_cur)])
                    P.op("pe", lambda e, ss=ss, first=first: e.matmul(ps[:, 4, :], lhsT=tri, rhs=spt[ss][:, 0:512],
                                                                      start=True, stop=first), reads=rd, writes=[("ps", 4)])
                    if not first:
                        P.op("pe", lambda e, cc=cs_cur: e.matmul(ps[:, 4, :], lhsT=ones, rhs=cst_[cc][:], start=False, stop=True),
                             reads=rd, writes=[("ps", 4)])
                    P.op("pe", lambda e, ss=ss: e.matmul(ps[:, 5, :], lhsT=tri, rhs=spt[ss][:, 512:1024], start=True, stop=False),
                         reads=rd, writes=[("ps", 5)])
                    if first:
                        P.op("pe", lambda e, ss=ss: e.matmul(ps[:, 5, :], lhsT=ones, rhs=spt[ss][:, 0:512], start=False, stop=True),
                             reads=rd, writes=[("ps", 5)])
                        cm = None
                    else:
                        cm = csr.next()
                        P.op("dve", lambda e, cm=cm, ss=ss, cc=cs_cur: e.tensor_tensor(out=cst_[cm][:], in0=cst_[cc][:],
                                                                                      in1=spt[ss][:, 0:512], op=ALU.add),
                             reads=[("sp", ss), ("csum", cs_cur)], writes=[("csum", cm)])
                        P.op("pe", lambda e, cm=cm: e.matmul(ps[:, 5, :], lhsT=ones, rhs=cst_[cm][:], start=False, stop=True),
                             reads=[("csum", cm)], writes=[("ps", 5)])
                    if c1 > 1:
                        cn = csr.next()
                        if first:
                            P.op("pool", lambda e, cn=cn, ss=ss: e.tensor_tensor(out=cst_[cn][:], in0=spt[ss][:, 0:512],
                                                                                in1=spt[ss][:, 512:1024], op=ALU.add),
                                 reads=[("sp", ss)], writes=[("csum", cn)])
                        else:
                            P.op("pool", lambda e, cn=cn, cm=cm, ss=ss: e.tensor_tensor(out=cst_[cn][:], in0=cst_[cm][:],
                                                                                       in1=spt[ss][:, 512:1024], op=ALU.add),
                                 reads=[("sp", ss), ("csum", cm)], writes=[("csum", cn)])
                        cs_cur = cn
                    P.op("act", lambda e, xs=xs: e.activation(out=xct[xs][:].rearrange("p (a b) -> p a b", a=2), in_=ps[:, 4:6, :],
                                                              func=AF.Exp, scale=-1.0),
                         reads=[("ps", 4), ("ps", 5)], writes=[("xc", xs)])
                    d["xs"] = xs
                if t >= 3:
                    d = info[t - 3]
                    qt, c1, es, xs = d["qt"], d["c1"], d["es"], d["xs"]
                    first = (c1 == nch(qt) - 1)
                    as_ = ar.next()
                    P.op("dve", lambda e, as_=as_, xs=xs, es=es: e.tensor_tensor(out=At[as_][:], in0=Et[es][:], in1=xct[xs][:],
                                                                                op=ALU.mult),
                         reads=[("E", es), ("xc", xs)], writes=[("A", as_)])
                    for h in range(2):
                        c = c1 - h
                        P.op("pe", lambda e, c=c, as_=as_, h=h, first=first: e.matmul(
                            ps[0:64, 6, :], lhsT=Vsb[:, c, :], rhs=At[as_][:, h * 512:(h + 1) * 512],
                            start=(first and h == 0), stop=(c == 0)),
                            reads=[("A", as_), "Vsb"], writes=[("ps", 6)])
                    if c1 == 1:
                        os_ = osr.next()
                        P.op("act", lambda e, os_=os_: e.copy(out=osT[os_][:], in_=ps[0:64, 6, :]),
                             reads=[("ps", 6)], writes=[("osT", os_)])
                        P.op("sp", lambda e, os_=os_, hs=hs, qt=qt: e.dma_start(
                            out=o_dst(qt, 256 + hs * 64, 256 + (hs + 1) * 64),
                            in_=osT[os_][:]), reads=[("osT", os_)], sem="osT%d" % os_, ndma=1)
        P.flush()


def phase_C0(P, nc, pairs):
    with ExitStack() as st:
        A = Alloc(nc, st, "c0_")
        stg = [A.sb("stg%d" % i, [128, 4096], F32) for i in range(3)]
        cb = [A.sb("cb%d" % i, [128, 4096], BF16) for i in range(3)]
        r = Rot(3)
        ce = Rot(2)
        work = []
        for (src_d, dst_d, R, C) in pairs:
            nk = R // 128
            if C <= 4096:
                kg = max(1, 4096 // C)
                items = [(k0, min(kg, nk - k0), 0, C) for k0 in range(0, nk, kg)]
            else:
                half = C // 2
                assert half <= 4096
                items = [(k0, 1, c0, half) for k0 in range(nk) for c0 in (0, half)]
            sv = src_d.rearrange("(k p) c -> p k c", p=128)
            dv = dst_d.rearrange("(k p) c -> p k c", p=128)
            for (k0, nkk, c0, cw) in items:
                work.append((sv, dv, k0, nkk, c0, cw))
        slots = [r.next() for _ in work]

        def c0_load(i):
            sv, dv, k0, nkk, c0, cw = work[i]
            s = slots[i]
            sview = stg[s][:, 0:nkk * cw].rearrange("p (k c) -> p k c", k=nkk)
            P.op("sp", lambda e: e.dma_start(out=sview, in_=sv[:, k0:k0 + nkk, c0:c0 + cw]),
                 writes=[("stg", s)], sem="c0stg%d" % s, ndma=1)

        for i in range(min(2, len(work))):
            c0_load(i)
        for i, (sv, dv, k0, nkk, c0, cw) in enumerate(work):
            s = slots[i]
            n = nkk * cw
            cview = cb[s][:, 0:n].rearrange("p (k c) -> p k c", k=nkk)
            eng = ("dve", "pool")[ce.next()]
            P.op(eng, lambda e, s=s, n=n: e.tensor_copy(out=cb[s][:, 0:n], in_=stg[s][:, 0:n]),
                 reads=[("stg", s)], writes=[("cb", s)])
            if i + 2 < len(work):
                c0_load(i + 2)
            P.op("sp", lambda e, cview=cview, k0=k0, nkk=nkk, c0=c0, cw=cw, dv=dv: e.dma_start(
                out=dv[:, k0:k0 + nkk, c0:c0 + cw], in_=cview), reads=[("cb", s)], sem="c0cb%d" % s, ndma=1)
        P.flush()


def phase_C1(P, nc, NT, o_recv_d, gT_d, x_d, WCO_d, WCOUT_d, WCI_d, WCFO_d, gffn_h, gffn_off, gfin_h, consts_d,
             xo_d, final, il=None):
    with ExitStack() as st:
        A = Alloc(nc, st, "c1_")
        cst = A.sb("cst", [128, NCONST], BF16)
        w_o = A.sb("w_o", [128, 8, 1024], BF16)
        w_out = A.sb("w_out", [128, 8, 1024], BF16)
        w_fo = A.sb("w_fo", [128, NF, 1024], BF16)
        gfb = A.sb("gfb", [128, 1024], F32)
        gnb = A.sb("gnb", [128, 1024], F32) if final else None
        odT = A.sb("odT", [128, 4, 512], BF16)
        osT = A.sb("osT", [128, 4, 512], BF16)
        gd = [A.sb("gd%d" % i, [128, 2, 512], BF16) for i in range(2)]
        m1 = [A.sb("m1_%d" % i, [128, 512], F32) for i in range(2)]
        m2 = [A.sb("m2_%d" % i, [128, 512], F32) for i in range(2)]
        mT = A.sb("mT", [128, 8, 512], BF16)
        xm = A.sb("xm", [128, 4, 1024], F32)
        junk = A.sb("junk", [128, 1024], BF16)
        ssq = A.sb("ssq", [128, 1], F32)
        rstd = A.sb("rstd", [128, 1], F32)
        hb = A.sb("hb", [128, 1024], BF16)
        h2T = A.sb("h2T", [128, 8, 512], BF16)
        aT = A.sb("aT", [128, NF, 512], BF16)
        sg = [A.sb("sg%d" % i, [128, 512], F32) for i in range(2)]
        wi = [A.sb("wi%d" % i, [128, 2, 8, 256], BF16) for i in range(2)]
        xo = [A.sb("xo%d" % i, [128, 1024], F32) for i in range(2)] if final else None
        ps = A.ps("ps", [128, 7, 512], F32)
        psT = A.ps("psT", [128, 1024], BF16)
        bank = Rot(7)
        ident = cst[:, C_ID:C_ID + 128]
        if il is not None:
            selt = A.sb("selt", [128, 2], F32)
            xb = A.sb("xb", [128, 2, 1024], F32)
            gd2 = [A.sb("gd2_%d" % i, [128, 2, 512], BF16) for i in range(2)]
            P.op("sp", lambda e: e.dma_start(out=selt[:], in_=bcast_ap(il["sel"], 2)), writes=["selt"], sem="selt", ndma=1)

        P.op("sp", lambda e: e.dma_start(out=cst[:], in_=consts_d), writes=["consts"], sem="cst", ndma=1)
        P.op("sp", lambda e: e.dma_start(out=w_o[:], in_=WCO_d.rearrange("(k p) c -> p k c", p=128)), writes=["w_o"],
             sem="w_o", ndma=1)
        P.op("sp", lambda e: e.dma_start(out=w_out[:], in_=WCOUT_d.rearrange("(k p) c -> p k c", p=128)), writes=["w_out"],
             sem="w_out", ndma=1)
        P.op("sp", lambda e: e.dma_start(out=w_fo[:], in_=WCFO_d.rearrange("(k p) c -> p k c", p=128)), writes=["w_fo"],
             sem="w_fo", ndma=1)
        P.op("sp", lambda e: e.dma_start(out=gfb[:], in_=bcast_ap(gffn_h, 1024, gffn_off)), writes=["gb"], sem="gfb", ndma=1)
        if final:
            P.op("sp", lambda e: e.dma_start(out=gnb[:], in_=bcast_ap(gfin_h, 1024)), writes=["gnb"], sem="gnb", ndma=1)
        WI_v = WCI_d.rearrange("(k p) c -> p k c", p=128)
        if il is None:
            gT_v = gT_d.rearrange("(h c p) t -> p h c t", h=2, p=128)
        else:
            gT_vs = [gT_d[th].rearrange("(h c p) t -> p h c t", h=2, p=128) for th in range(2)]
        ntt = NT // 512
        wir = Rot(2)
        def ld_o(t0):
            P.op("sp", lambda e, t0=t0: [e.dma_start(out=odT[:, 2 * r:2 * r + 2, :],
                                                     in_=o_recv_d[r, 0:256, t0:t0 + 512].rearrange("(c p) t -> p c t", p=128))
                                         for r in range(2)], writes=["odT"], sem="odT", ndma=2)
            P.op("sp", lambda e, t0=t0: [e.dma_start(out=osT[:, 2 * r:2 * r + 2, :],
                                                     in_=o_recv_d[r, 256:512, t0:t0 + 512].rearrange("(c p) t -> p c t", p=128))
                                         for r in range(2)], writes=["osT"], sem="osT", ndma=2)

        ld_o(0)
        for tb in range(NT // 512):
            t0 = tb * 512
            xmk = [("xm", ts, ch) for ts in range(4) for ch in range(2)]
            if il is None:
                P.op("sp", lambda e, t0=t0: e.dma_start(out=xm[:], in_=x_d[t0:t0 + 512, :].rearrange("(s p) c -> p s c", p=128)),
                     writes=xmk, sem="xm", ndma=1)
            else:
                ra, rb = 2 * tb * 512, (2 * tb + 1) * 512
                P.op("sp", lambda e, ra=ra: e.dma_start(out=xm[:], in_=x_d[ra:ra + 512, :].rearrange("(s p) c -> p s c", p=128)),
                     writes=xmk, sem="xm", ndma=1)
                P.op("dve", lambda e: e.tensor_scalar(out=xm[:], in0=xm[:], scalar1=selt[:, 0:1], scalar2=None, op0=ALU.mult),
                     reads=xmk + ["selt"], writes=xmk)
                for hb_ in range(2):
                    P.op("sp", lambda e, rb=rb, hb_=hb_: e.dma_start(
                        out=xb[:], in_=x_d[rb + hb_ * 256:rb + (hb_ + 1) * 256, :].rearrange("(s p) c -> p s c", p=128)),
                        writes=["xb"], sem="xb", ndma=1)
                    kk = [("xm", ts, ch) for ts in range(2 * hb_, 2 * hb_ + 2) for ch in range(2)]
                    P.op("dve", lambda e, hb_=hb_: e.scalar_tensor_tensor(
                        out=xm[:, 2 * hb_:2 * hb_ + 2, :], in0=xb[:], scalar=selt[:, 1:2], in1=xm[:, 2 * hb_:2 * hb_ + 2, :],
                        op0=ALU.mult, op1=ALU.add), reads=["xb", "selt"] + kk, writes=kk)
            for j in range(8):
                s = j % 2
                if il is None:
                    P.op("sp", lambda e, j=j, s=s, t0=t0: e.dma_start(out=gd[s][:], in_=gT_v[:, :, j, t0:t0 + 512]),
                         writes=[("gd", s)], sem="gd%d" % s, ndma=1)
                else:
                    th_, tt0 = (2 * tb) // ntt, (2 * tb) % ntt
                    gv = gT_vs[th_]
                    P.op("sp", lambda e, j=j, s=s, gv=gv, tt0=tt0: e.dma_start(out=gd[s][:], in_=gv[:, :, j, tt0 * 512:(tt0 + 1) * 512]),
                         writes=[("gd", s)], sem="gd%d" % s, ndma=1)
                    P.op("sp", lambda e, j=j, s=s, gv=gv, tt0=tt0: e.dma_start(out=gd2[s][:],
                                                                              in_=gv[:, :, j, (tt0 + 1) * 512:(tt0 + 2) * 512]),
                         writes=[("gd2", s)], sem="gd2_%d" % s, ndma=1)
                    P.op("pool", lambda e, s=s: e.tensor_scalar(out=gd[s][:], in0=gd[s][:], scalar1=selt[:, 0:1], scalar2=None,
                                                                op0=ALU.mult), reads=[("gd", s), "selt"], writes=[("gd", s)])
                    P.op("dve", lambda e, s=s: e.scalar_tensor_tensor(out=gd[s][:], in0=gd2[s][:], scalar=selt[:, 1:2], in1=gd[s][:],
                                                                      op0=ALU.mult, op1=ALU.add),
                         reads=[("gd", s), ("gd2", s), "selt"], writes=[("gd", s)])
                b1, b2 = bank.next(), bank.next()
                for kc in range(4):
                    P.op("pe", lambda e, kc=kc, b1=b1, j=j: e.matmul(ps[:, b1, :], lhsT=w_o[:, kc, j * 128:(j + 1) * 128],
                                                                     rhs=odT[:, kc, :], start=(kc == 0), stop=(kc == 3)),
                         reads=["w_o", "odT"], writes=[("ps", b1)])
                for kc in range(4):
                    P.op("pe", lambda e, kc=kc, b2=b2, j=j: e.matmul(ps[:, b2, :], lhsT=w_o[:, 4 + kc, j * 128:(j + 1) * 128],
                                                                     rhs=osT[:, kc, :], start=(kc == 0), stop=(kc == 3)),
                         reads=["w_o", "osT"], writes=[("ps", b2)])
                P.op("dve", lambda e, b1=b1, s=s: e.tensor_tensor(out=m1[s][:], in0=ps[:, b1, :], in1=gd[s][:, 0, :], op=ALU.mult),
                     reads=[("ps", b1), ("gd", s)], writes=[("m1", s)])
                P.op("dve", lambda e, b2=b2, s=s: e.tensor_tensor(out=m2[s][:], in0=ps[:, b2, :], in1=gd[s][:, 1, :], op=ALU.mult),
                     reads=[("ps", b2), ("gd", s)], writes=[("m2", s)])
                P.op("pool", lambda e, j=j, s=s: e.tensor_tensor(out=mT[:, j, :], in0=m1[s][:], in1=m2[s][:], op=ALU.add),
                     reads=[("m1", s), ("m2", s)], writes=[("mT", j)])
            if tb + 1 < NT // 512:
                ld_o(t0 + 512)
            for ts in range(4):
                for ch in range(2):
                    b = bank.next()
                    for k in range(8):
                        P.op("pe", lambda e, k=k, b=b, ts=ts, ch=ch: e.matmul(
                            ps[:, b, :], lhsT=mT[:, k, ts * 128:(ts + 1) * 128], rhs=w_out[:, k, ch * 512:(ch + 1) * 512],
                            start=(k == 0), stop=(k == 7)),
                            reads=[("mT", k), "w_out"], writes=[("ps", b)])
                    P.op("dve", lambda e, b=b, ts=ts, ch=ch: e.tensor_tensor(
                        out=xm[:, ts, ch * 512:(ch + 1) * 512], in0=ps[:, b, :], in1=xm[:, ts, ch * 512:(ch + 1) * 512], op=ALU.add),
                        reads=[("ps", b), ("xm", ts, ch)], writes=[("xm", ts, ch)])
            for ts in range(4):
                emit_norm_transpose2(P, [("xm", ts, 0), ("xm", ts, 1)], xm[:, ts, :], gfb, junk, ssq, rstd, hb, psT, ident,
                                     h2T[:, :, ts * 128:(ts + 1) * 128], [("h2T", ts)])
            for g in range(NF // 2):
                w = wir.next()
                f0 = g * 2
                P.op("sp", lambda e, w=w, f0=f0: [e.dma_start(out=wi[w][:, 0, :, :], in_=WI_v[:, :, f0 * 128:f0 * 128 + 256]),
                                                  e.dma_start(out=wi[w][:, 1, :, :],
                                                              in_=WI_v[:, :, FFN + f0 * 128:FFN + f0 * 128 + 256])],
                     writes=[("wi", w)], sem="wi%d" % w, ndma=2)
                for fl in range(2):
                    f = f0 + fl
                    bg_, bu_ = bank.next(), bank.next()
                    for (bb_, gu) in ((bg_, 0), (bu_, 1)):
                        for k in range(8):
                            P.op("pe", lambda e, k=k, bb_=bb_, gu=gu, fl=fl, w=w: e.matmul(
                                ps[:, bb_, :], lhsT=wi[w][:, gu, k, fl * 128:(fl + 1) * 128], rhs=h2T[:, k, :],
                                start=(k == 0), stop=(k == 7)),
                                reads=[("wi", w)] + [("h2T", ts) for ts in range(4)], writes=[("ps", bb_)])
                    s = f % 2
                    P.op("act", lambda e, bg_=bg_, s=s: e.activation(out=sg[s][:], in_=ps[:, bg_, :], func=AF.Silu),
                         reads=[("ps", bg_)], writes=[("sg", s)])
                    P.op("dve", lambda e, bu_=bu_, s=s, f=f: e.tensor_tensor(out=aT[:, f, :], in0=ps[:, bu_, :], in1=sg[s][:],
                                                                             op=ALU.mult),
                         reads=[("ps", bu_), ("sg", s)], writes=[("aT", f)])
            for ts in range(4):
                for ch in range(2):
                    b = bank.next()
                    for f in range(NF):
                        P.op("pe", lambda e, f=f, b=b, ts=ts, ch=ch: e.matmul(
                            ps[:, b, :], lhsT=aT[:, f, ts * 128:(ts + 1) * 128], rhs=w_fo[:, f, ch * 512:(ch + 1) * 512],
                            start=(f == 0), stop=(f == NF - 1)),
                            reads=[("aT", f), "w_fo"], writes=[("ps", b)])
                    P.op("dve", lambda e, b=b, ts=ts, ch=ch: e.tensor_tensor(
                        out=xm[:, ts, ch * 512:(ch + 1) * 512], in0=ps[:, b, :], in1=xm[:, ts, ch * 512:(ch + 1) * 512], op=ALU.add),
                        reads=[("ps", b), ("xm", ts, ch)], writes=[("xm", ts, ch)])
            if not final:
                P.op("act", lambda e, t0=t0: e.dma_start(out=xo_d[t0:t0 + 512, :].rearrange("(s p) c -> p s c", p=128), in_=xm[:]),
                     reads=[("xm", ts, ch) for ts in range(4) for ch in range(2)], sem="xm", ndma=1)
            else:
                for ts in range(4):
                    xs = ts % 2
                    xk = [("xm", ts, 0), ("xm", ts, 1)]
                    P.op("act", lambda e, ts=ts: e.activation(out=junk[:], in_=xm[:, ts, :], func=AF.Square, accum_out=ssq[:]),
                         reads=xk, writes=["junk", "ssq"])
                    P.op("act", lambda e: e.activation(out=rstd[:], in_=ssq[:], func=AF.Ln, scale=1.0 / D, bias=EPS),
                         reads=["ssq"], writes=["rstd"])
                    P.op("act", lambda e: e.activation(out=rstd[:], in_=rstd[:], func=AF.Exp, scale=-0.5),
                         reads=["rstd"], writes=["rstd"])
                    P.op("dve", lambda e, ts=ts, xs=xs: e.scalar_tensor_tensor(out=xo[xs][:], in0=xm[:, ts, :], scalar=rstd[:, 0:1],
                                                                               in1=gnb[:], op0=ALU.mult, op1=ALU.mult),
                         reads=xk + ["rstd", "gnb"], writes=[("xo", xs)])
                    P.op("act", lambda e, ts=ts, xs=xs, t0=t0: e.dma_start(out=xo_d[t0 + ts * 128:t0 + (ts + 1) * 128, :], in_=xo[xs][:]),
                         reads=[("xo", xs)], sem="xo%d" % xs, ndma=1)
        P.flush()


def emit_norm_transpose2(P, xkeys, x_ap, gb, junk, ssq, rstd, hb, psT, ident, out_ap, out_keys):
    P.op("act", lambda e: e.activation(out=junk[:], in_=x_ap, func=AF.Square, accum_out=ssq[:]),
         reads=xkeys, writes=["junk", "ssq"])
    P.op("act", lambda e: e.activation(out=rstd[:], in_=ssq[:], func=AF.Ln, scale=1.0 / D, bias=EPS),
         reads=["ssq"], writes=["rstd"])
    P.op("act", lambda e: e.activation(out=rstd[:], in_=rstd[:], func=AF.Exp, scale=-0.5),
         reads=["rstd"], writes=["rstd"])
    P.op("dve", lambda e: e.scalar_tensor_tensor(out=hb[:], in0=x_ap, scalar=rstd[:, 0:1], in1=gb[:],
                                                 op0=ALU.mult, op1=ALU.mult),
         reads=list(xkeys) + ["rstd", "gb"], writes=["hb"])
    for j in range(8):
        P.op("pe", lambda e, j=j: e.transpose(out=psT[:, j * 128:(j + 1) * 128], in_=hb[:, j * 128:(j + 1) * 128],
                                             identity=ident),
             reads=["hb", "consts"], writes=["psT"])
    P.op("act", lambda e: e.copy(out=out_ap, in_=psT[:].rearrange("p (c t) -> p c t", c=8)),
         reads=["psT"], writes=out_keys)


def _nc():
    return bass.Bass("TRN2", target_bir_lowering=False)


def build_LA(S):
    NT = S // 2
    nc = _nc()
    x = nc.dram_tensor("x", [NT, D], F32, kind="ExternalInput")
    g = nc.dram_tensor("g", [D], F32, kind="ExternalInput")
    consts = nc.dram_tensor("consts", [128, NCONST], BF16, kind="ExternalInput")
    hT = nc.dram_tensor("hT", [D, NT], BF16, kind="ExternalOutput")
    with ExitStack() as st:
        P = Prog(nc, st)
        phase_A1(P, nc, NT, x.ap(), g, consts.ap(), hT.ap())
    return nc


def build_LB(S):
    NT = S // 2
    nc = _nc()
    hT_all = nc.dram_tensor("hT_all", [2, D, NT], BF16, kind="ExternalInput")
    hT_mine = nc.dram_tensor("hT_mine", [D, NT], BF16, kind="ExternalInput")
    wqk = nc.dram_tensor("wqk", [D, 1024], F32, kind="ExternalInput")
    wv = nc.dram_tensor("wv", [D, 512], F32, kind="ExternalInput")
    wg = nc.dram_tensor("wg", [D, 2048], F32, kind="ExternalInput")
    bg = nc.dram_tensor("bg", [128, 16], F32, kind="ExternalInput")
    cos = nc.dram_tensor("cos", [128, S], F32, kind="ExternalInput")
    sin = nc.dram_tensor("sin", [128, S], F32, kind="ExternalInput")
    lam = nc.dram_tensor("lam", [256], F32, kind="ExternalInput")
    subln = nc.dram_tensor("subln", [128], F32, kind="ExternalInput")
    lcfg = nc.dram_tensor("lcfg", [2], F32, kind="ExternalInput")
    consts = nc.dram_tensor("consts", [128, NCONST], BF16, kind="ExternalInput")
    QK = nc.dram_tensor("QK", [1024, S], BF16, kind="Internal")
    V = nc.dram_tensor("V", [S, 512], BF16, kind="Internal")
    o_send = nc.dram_tensor("o_send", [2, 512, NT], BF16, kind="ExternalOutput")
    gT = nc.dram_tensor("gT", [2048, NT], BF16, kind="ExternalOutput")
    with ExitStack() as st:
        P = Prog(nc, st)
        phase_A2(P, nc, S, hT_all.ap(), hT_mine.ap(), wqk.ap(), wv.ap(), wg.ap(), bg.ap(), cos.ap(), sin.ap(),
                 QK.ap(), V.ap(), gT.ap())
        phase_B(P, nc, S, QK.ap(), V.ap(), lam, 0, subln, 0, lcfg, consts.ap(), o_send.ap())
    return nc


def build_LC(S, final):
    NT = S // 2
    nc = _nc()
    o_recv = nc.dram_tensor("o_recv", [2, 512, NT], BF16, kind="ExternalInput")
    gT = nc.dram_tensor("gT", [2048, NT], BF16, kind="ExternalInput")
    x = nc.dram_tensor("x", [NT, D], F32, kind="ExternalInput")
    wo = nc.dram_tensor("wo", [1024, D], F32, kind="ExternalInput")
    wout = nc.dram_tensor("wout", [D, D], F32, kind="ExternalInput")
    wfi = nc.dram_tensor("wfi", [D, 2 * FFN], F32, kind="ExternalInput")
    wfo = nc.dram_tensor("wfo", [FFN, D], F32, kind="ExternalInput")
    gffn = nc.dram_tensor("gffn", [D], F32, kind="ExternalInput")
    gfin = nc.dram_tensor("gfin", [D], F32, kind="ExternalInput")
    consts = nc.dram_tensor("consts", [128, NCONST], BF16, kind="ExternalInput")
    WCO = nc.dram_tensor("WCO", [1024, D], BF16, kind="Internal")
    WCOUT = nc.dram_tensor("WCOUT", [D, D], BF16, kind="Internal")
    WCI = nc.dram_tensor("WCI", [D, 2 * FFN], BF16, kind="Internal")
    WCFO = nc.dram_tensor("WCFO", [FFN, D], BF16, kind="Internal")
    xo = nc.dram_tensor("xo", [NT, D], F32, kind="ExternalOutput")
    with ExitStack() as st:
        P = Prog(nc, st)
        phase_C0(P, nc, [(wo.ap(), WCO.ap(), 1024, D), (wout.ap(), WCOUT.ap(), D, D), (wfi.ap(), WCI.ap(), D, 2 * FFN),
                         (wfo.ap(), WCFO.ap(), FFN, D)])
        phase_C1(P, nc, NT, o_recv.ap(), gT.ap(), x.ap(), WCO.ap(), WCOUT.ap(), WCI.ap(), WCFO.ap(), gffn, 0, gfin,
                 consts.ap(), xo.ap(), final)
    return nc


def _run(nc, in_maps):
    res = run_bass_kernel_spmd(nc, in_maps, core_ids=list(range(N_CORES)))
    return res.results


def _f32(a):
    return np.ascontiguousarray(np.asarray(a, dtype=np.float32))


def slice_weights(inp, l, hf):
    w_in = inp["w_in"][l]
    dq = w_in[:, hf * 256:(hf + 1) * 256]
    dk = w_in[:, 512 + hf * 256:512 + (hf + 1) * 256]
    dv = w_in[:, 1024 + hf * 256:1024 + (hf + 1) * 256]
    sq = w_in[:, 1536 + hf * 256:1536 + (hf + 1) * 256]
    sk = w_in[:, 2048 + hf * 256:2048 + (hf + 1) * 256]
    sv = w_in[:, 2560 + hf * 256:2560 + (hf + 1) * 256]
    return dict(wqk=_f32(np.concatenate([dq, dk, sq, sk], axis=1)), wv=_f32(np.concatenate([dv, sv], axis=1)),
                wg=_f32(w_in[:, 3072:5120]))


def kernel_unfused(inp, S):
    NT = S // 2
    B = inp["x"].shape[0]
    assert 2 * B == N_CORES
    consts = host_consts()
    cosT, sinT = rope_tables(S)
    xs = [_f32(inp["x"][c // 2, (c % 2) * NT:(c % 2 + 1) * NT]) for c in range(N_CORES)]
    ncA = build_LA(S)
    ncB = build_LB(S)
    for l in range(2):
        lam_init = 0.8 - 0.6 * math.exp(-0.3 * l)
        resA = _run(ncA, [dict(x=xs[c], g=_f32(inp["norm_attn"][l]), consts=consts) for c in range(N_CORES)])
        hT = [resA[c]["hT"] for c in range(N_CORES)]
        mapsB = []
        for c in range(N_CORES):
            b, hf = c // 2, c % 2
            m = slice_weights(inp, l, hf)
            m.update(hT_all=np.stack([hT[2 * b], hT[2 * b + 1]]), hT_mine=hT[c],
                     bg=_f32(np.asarray(inp["b_gate"][l]).reshape(16, 128).T),
                     cos=cosT, sin=sinT, lam=_f32(np.asarray(inp["diff_lambda"][l]).reshape(256)),
                     subln=_f32(inp["diff_subln"][l]), lcfg=np.array([-lam_init, 1.0 - lam_init], np.float32),
                     consts=consts)
            mapsB.append(m)
        resB = _run(ncB, mapsB)
        ncC = build_LC(S, final=(l == 1))
        mapsC = []
        for c in range(N_CORES):
            b, hf = c // 2, c % 2
            o_recv = np.stack([resB[2 * b + r]["o_send"][hf] for r in range(2)])
            mapsC.append(dict(o_recv=np.ascontiguousarray(o_recv), gT=resB[c]["gT"], x=xs[c],
                              wo=_f32(np.concatenate([inp["w_o_diff"][l], inp["w_o_sb"][l]], axis=0)),
                              wout=_f32(inp["w_out"][l]), wfi=_f32(inp["w_ffn_in"][l]), wfo=_f32(inp["w_ffn_out"][l]),
                              gffn=_f32(inp["norm_ffn"][l]), gfin=_f32(inp["norm_final"]), consts=consts))
        resC = _run(ncC, mapsC)
        xs = [resC[c]["xo"] for c in range(N_CORES)]
    out = np.zeros((B, S, D), np.float32)
    for c in range(N_CORES):
        out[c // 2, (c % 2) * NT:(c % 2 + 1) * NT] = xs[c]
    return out


def phase_XSEL(P, nc, NT, x_d, sel_h, xs_d):
    with ExitStack() as st:
        A = Alloc(nc, st, "xs_")
        selt = A.sb("selt", [128, 2], F32)
        xa = [A.sb("xa%d" % i, [128, 4, 1024], F32) for i in range(2)]
        xb = [A.sb("xb%d" % i, [128, 4, 1024], F32) for i in range(2)]
        P.op("sp", lambda e: e.dma_start(out=selt[:], in_=bcast_ap(sel_h, 2)), writes=["selt"], sem="selt", ndma=1)
        for j in range(NT // 512):
            s = j % 2
            ra, rb = 2 * j * 512, (2 * j + 1) * 512
            P.op("sp", lambda e, ra=ra, s=s: e.dma_start(out=xa[s][:], in_=x_d[ra:ra + 512, :].rearrange("(s p) c -> p s c", p=128)),
                 writes=[("xa", s)], sem="xa%d" % s, ndma=1)
            P.op("sp", lambda e, rb=rb, s=s: e.dma_start(out=xb[s][:], in_=x_d[rb:rb + 512, :].rearrange("(s p) c -> p s c", p=128)),
                 writes=[("xb", s)], sem="xb%d" % s, ndma=1)
            P.op("pool", lambda e, s=s: e.tensor_scalar(out=xa[s][:], in0=xa[s][:], scalar1=selt[:, 0:1], scalar2=None, op0=ALU.mult),
                 reads=[("xa", s), "selt"], writes=[("xa", s)])
            P.op("dve", lambda e, s=s: e.scalar_tensor_tensor(out=xa[s][:], in0=xb[s][:], scalar=selt[:, 1:2], in1=xa[s][:],
                                                              op0=ALU.mult, op1=ALU.add),
                 reads=[("xa", s), ("xb", s), "selt"], writes=[("xa", s)])
            P.op("act", lambda e, j=j, s=s: e.dma_start(out=xs_d[j * 512:(j + 1) * 512, :].rearrange("(s p) c -> p s c", p=128),
                                                        in_=xa[s][:]), reads=[("xa", s)], sem="xa%d" % s, ndma=1)
        P.flush()


def build_fused(S):
    NT = S // 2
    nc = _nc()
    di = lambda name, shape, dt: nc.dram_tensor(name, shape, dt, kind="ExternalInput")
    dn = lambda name, shape, dt: nc.dram_tensor(name, shape, dt, kind="Internal")
    x = di("x", [S, D], F32)
    gattn = di("gattn", [2 * D], F32)
    wqk = di("wqk", [2, 2, D, 1024], F32)
    wv = di("wv", [2, 2, D, 512], F32)
    wg = di("wg", [2, D, 2048], F32)
    bg = di("bg", [2, 128, 16], F32)
    cos = di("cos", [128, S], F32)
    sin = di("sin", [128, S], F32)
    lam = di("lam", [2 * 256], F32)
    subln = di("subln", [2 * 128], F32)
    lcfg = di("lcfg", [2 * 2], F32)
    wo = di("wo", [2, 1024, D], F32)
    wout = di("wout", [2, D, D], F32)
    wfi = di("wfi", [2, D, 2 * FFN], F32)
    wfo = di("wfo", [2, FFN, D], F32)
    gffn = di("gffn", [2 * D], F32)
    gfin = di("gfin", [D], F32)
    consts = di("consts", [128, NCONST], BF16)
    sel = di("sel", [2], F32)
    mtI = di("mtI", [128, 8, 512], BF16)
    mtS = di("mtS", [128, 8, 512], BF16)
    o_all1 = dn("o_all1", [2, 512, NT], BF16)
    gT_sel = dn("gT_sel", [2048, NT], BF16)
    xsel = dn("xsel", [NT, D], F32)
    hT_all = dn("hT_all", [2, D, NT], BF16)
    QK = dn("QK", [1024, S], BF16)
    V = dn("V", [S, 512], BF16)
    o_all = dn("o_all", [2, 2, 512, NT], BF16)
    gT_all = dn("gT_all", [2, 2048, NT], BF16)
    x1 = dn("x1", [S, D], F32)
    WCO = dn("WCO", [1024, D], BF16)
    WCOUT = dn("WCOUT", [D, D], BF16)
    WCI = dn("WCI", [D, 2 * FFN], BF16)
    WCFO = dn("WCFO", [FFN, D], BF16)
    xo = nc.dram_tensor("xo", [NT, D], F32, kind="ExternalOutput")
    with ExitStack() as st:
        P = Prog(nc, st)
        for l in range(2):
            xin = x.ap() if l == 0 else x1.ap()
            ilB = None if l == 0 else dict(sel=sel, mtI=mtI.ap(), mtS=mtS.ap())
            for th in range(2):
                phase_A1(P, nc, NT, xin[th * NT:(th + 1) * NT, :], gattn, consts.ap(), hT_all.ap()[th], g_off=l * D)
            for hh in range(2):
                phase_A2(P, nc, S, hT_all.ap(), hT_all.ap()[hh], wqk.ap()[l, hh], wv.ap()[l, hh], wg.ap()[l], bg.ap()[l],
                         cos.ap(), sin.ap(), QK.ap(), V.ap(), gT_all.ap()[hh] if l == 0 else gT_sel.ap(),
                         il=None if l == 0 else dict(sel=sel, vt0=hh * (NT // 1024)))
                phase_B(P, nc, S, QK.ap(), V.ap(), lam, l * 256, subln, l * 128, (lcfg, l * 2), consts.ap(),
                        o_all.ap()[hh] if l == 0 else o_all1.ap()[hh], il=ilB)
            phase_C0(P, nc, [(wo.ap()[l], WCO.ap(), 1024, D), (wout.ap()[l], WCOUT.ap(), D, D),
                             (wfi.ap()[l], WCI.ap(), D, 2 * FFN), (wfo.ap()[l], WCFO.ap(), FFN, D)])
            if l == 0:
                for th in range(2):
                    phase_C1(P, nc, NT, o_all.ap()[:, th], gT_all.ap()[th], xin[th * NT:(th + 1) * NT, :], WCO.ap(), WCOUT.ap(),
                             WCI.ap(), WCFO.ap(), gffn, l * D, gfin, consts.ap(), x1.ap()[th * NT:(th + 1) * NT, :], final=False)
            else:
                phase_XSEL(P, nc, NT, xin, sel, xsel.ap())
                phase_C1(P, nc, NT, o_all1.ap(), gT_sel.ap(), xsel.ap(), WCO.ap(), WCOUT.ap(), WCI.ap(), WCFO.ap(), gffn, l * D,
                         gfin, consts.ap(), xo.ap(), final=True)
    return nc


def kernel_fused(inp, S):
    B = inp["x"].shape[0]
    assert 2 * B == N_CORES
    consts = host_consts()
    cosT, sinT = rope_tables(S)
    sl = [[slice_weights(inp, l, hh) for hh in range(2)] for l in range(2)]
    shared = dict(
        gattn=_f32(np.asarray(inp["norm_attn"]).reshape(-1)),
        wqk=_f32(np.stack([np.stack([sl[l][hh]["wqk"] for hh in range(2)]) for l in range(2)])),
        wv=_f32(np.stack([np.stack([sl[l][hh]["wv"] for hh in range(2)]) for l in range(2)])),
        wg=_f32(np.stack([sl[l][0]["wg"] for l in range(2)])),
        bg=_f32(np.stack([np.asarray(inp["b_gate"][l]).reshape(16, 128).T for l in range(2)])),
        cos=cosT, sin=sinT,
        lam=_f32(np.asarray(inp["diff_lambda"]).reshape(-1)),
        subln=_f32(np.asarray(inp["diff_subln"]).reshape(-1)),
        lcfg=np.array([v for l in range(2) for v in (-(0.8 - 0.6 * math.exp(-0.3 * l)), 1.0 - (0.8 - 0.6 * math.exp(-0.3 * l)))],
                      np.float32),
        wo=_f32(np.concatenate([inp["w_o_diff"], inp["w_o_sb"]], axis=1)),
        wout=_f32(inp["w_out"]), wfi=_f32(inp["w_ffn_in"]), wfo=_f32(inp["w_ffn_out"]),
        gffn=_f32(np.asarray(inp["norm_ffn"]).reshape(-1)), gfin=_f32(inp["norm_final"]), consts=consts)
    nc = build_fused(S)
    jj = np.arange(128)[:, None]
    tt = np.arange(512)[None, :]
    mt = {}
    for hf in range(2):
        mi = np.zeros((128, 8, 512), np.float32)
        ms = np.zeros((128, 8, 512), np.float32)
        for m_ in range(8):
            off = m_ * 128 - hf * 512
            mi[:, m_, :] = (tt - off - jj >= 0)
            ms[:, m_, :] = (tt - off - jj > 0)
        mt[hf] = (mi.astype(NPBF), ms.astype(NPBF))
    maps = []
    for c in range(N_CORES):
        hf = c % 2
        m = dict(shared)
        m["x"] = _f32(inp["x"][c // 2])
        m["sel"] = np.array([1.0 - hf, float(hf)], np.float32)
        m["mtI"], m["mtS"] = mt[hf]
        maps.append(m)
    res = _run(nc, maps)
    NT = S // 2
    out = np.zeros((B, S, D), np.float32)
    for c in range(N_CORES):
        b, hf = c // 2, c % 2
        xo = res[c]["xo"]
        for j in range(NT // 512):
            out[b, (2 * j + hf) * 512:(2 * j + hf + 1) * 512] = xo[j * 512:(j + 1) * 512]
    return out


def kernel(**inputs):
    inp = {k: np.asarray(v) for k, v in inputs.items()}
    return kernel_fused(inp, inp["x"].shape[1])
```
